# Optimizing a Trainium2 kernel written in Bass

```python
import math
import jax, jax.numpy as jnp
from jax import lax
import numpy as np

D_MODEL = 2048
BATCH = 8
SEQ = 2048
DEPTH = 1

N_META = 16
N_HEADS = 8
N_KV_HEADS = 2
HEAD_DIM = 128
ATTN_WIDTH = N_HEADS * HEAD_DIM
KV_WIDTH = N_KV_HEADS * HEAD_DIM
N_IDX_HEADS = 16
IDX_DIM = 64
TOPK_MAX = 256
CONV_WIDTH = D_MODEL // 2
CONV_K = 3
D_FF = 5632
ROPE_THETA = 500000.0
ROT_DIV = 4
Q_BLOCK = 128
EPS = 1e-6

kernel_name = "hybrid_dsa_shortconv_macaron_block"

SPLITS = [ATTN_WIDTH, KV_WIDTH, KV_WIDTH, N_IDX_HEADS * IDX_DIM, IDX_DIM, N_IDX_HEADS,
          CONV_WIDTH, CONV_WIDTH, CONV_WIDTH, D_MODEL, D_MODEL]
IN_COLS = int(sum(SPLITS))
SPLIT_POINTS = [int(v) for v in np.cumsum(SPLITS)[:-1]]


def rms_norm(x, g):
    xf = x.astype(jnp.float32)
    y = xf * lax.rsqrt(jnp.mean(xf * xf, axis=-1, keepdims=True) + EPS)
    return (y * g.astype(jnp.float32)).astype(x.dtype)


def swiglu(x, w_gate, w_up, w_down):
    return (jax.nn.silu(x @ w_gate) * (x @ w_up)) @ w_down


def rope_tables(n_pos, rot_dim):
    inv = ROPE_THETA ** (-jnp.arange(0, rot_dim, 2, dtype=jnp.float32) / rot_dim)
    ang = jnp.arange(n_pos, dtype=jnp.float32)[:, None] * inv[None, :]
    return jnp.cos(ang), jnp.sin(ang)


def partial_rope(x, cos, sin):
    half = cos.shape[-1]
    rot = 2 * half
    x1 = x[..., :half].astype(jnp.float32)
    x2 = x[..., half:rot].astype(jnp.float32)
    c = cos[None, :, None, :]
    s = sin[None, :, None, :]
    r1 = (x1 * c - x2 * s).astype(x.dtype)
    r2 = (x2 * c + x1 * s).astype(x.dtype)
    return jnp.concatenate([r1, r2, x[..., rot:]], axis=-1)


def causal_depthwise_conv(x, w, b):
    y = lax.conv_general_dilated(
        x, w[:, None, :].astype(x.dtype), window_strides=(1,),
        padding=[(CONV_K - 1, 0)], dimension_numbers=('NWC', 'WIO', 'NWC'),
        feature_group_count=x.shape[-1])
    return y + b.astype(x.dtype)


def dsa_sparse_attention(q, k, v, q_idx, k_idx, w_idx, k_sel):
    B, T = q.shape[0], q.shape[1]
    n_blk = -(-T // Q_BLOCK)
    Tp = n_blk * Q_BLOCK
    pad = Tp - T
    rep = N_HEADS // N_KV_HEADS

    def to_blocks(a):
        a = jnp.pad(a, [(0, 0), (0, pad)] + [(0, 0)] * (a.ndim - 2))
        return jnp.moveaxis(a.reshape((B, n_blk, Q_BLOCK) + a.shape[2:]), 1, 0)

    qpos = jnp.arange(Tp, dtype=jnp.int32).reshape(n_blk, Q_BLOCK)
    kpos = jnp.arange(T, dtype=jnp.int32)

    def block(args):
        qb, qib, wb, pb = args
        causal = kpos[None, :] <= pb[:, None]
        dots = jnp.einsum('bqhd,bsd->bqhs', qib, k_idx)
        isc = jnp.einsum('bqhs,bqh->bqs', jax.nn.relu(dots), wb).astype(jnp.float32)
        isc = jnp.where(causal[None], isc, -jnp.inf)
        _, sel = lax.top_k(isc, k_sel)
        valid = sel <= pb[None, :, None]
        ks = jax.vmap(lambda a, i: a[i])(k, sel)
        vs = jax.vmap(lambda a, i: a[i])(v, sel)
        qg = qb.reshape(B, Q_BLOCK, N_KV_HEADS, rep, HEAD_DIM)
        s = jnp.einsum('bqgrd,bqkgd->bqgrk', qg, ks).astype(jnp.float32) * (HEAD_DIM ** -0.5)
        s = jnp.where(valid[:, :, None, None, :], s, -jnp.inf)
        p = jax.nn.softmax(s, axis=-1).astype(vs.dtype)
        o = jnp.einsum('bqgrk,bqkgd->bqgrd', p, vs)
        return o.reshape(B, Q_BLOCK, ATTN_WIDTH)

    out = lax.map(block, (to_blocks(q), to_blocks(q_idx), to_blocks(w_idx), qpos))
    out = jnp.moveaxis(out, 0, 1).reshape(B, Tp, ATTN_WIDTH)[:, :T]
    return out


def setup_inputs(seed: int = 0) -> dict:
    key = jax.random.key(seed)
    ks = jax.random.split(key, 20)
    f32 = jnp.float32

    def w(k, shape, fan_in):
        return jax.random.normal(k, shape, f32) * (fan_in ** -0.5)

    def gain(k, shape):
        return 1.0 + 0.02 * jax.random.normal(k, shape, f32)

    L = DEPTH
    return {
        "x": jax.random.normal(ks[0], (BATCH, SEQ, D_MODEL), f32),
        "meta_tokens": jax.random.normal(ks[1], (N_META, D_MODEL), f32),
        "ffn1_norm_g": gain(ks[2], (L, D_MODEL)),
        "ffn1_w_gate": w(ks[3], (L, D_MODEL, D_FF), D_MODEL),
        "ffn1_w_up": w(ks[4], (L, D_MODEL, D_FF), D_MODEL),
        "ffn1_w_down": w(ks[5], (L, D_FF, D_MODEL), D_FF),
        "mix_norm_g": gain(ks[6], (L, D_MODEL)),
        "w_in": w(ks[7], (L, D_MODEL, IN_COLS), D_MODEL),
        "q_norm_g": gain(ks[8], (L, HEAD_DIM)),
        "k_norm_g": gain(ks[9], (L, HEAD_DIM)),
        "conv_w": w(ks[10], (L, CONV_K, CONV_WIDTH), CONV_K),
        "conv_b": 0.01 * jax.random.normal(ks[11], (L, CONV_WIDTH), f32),
        "w_attn_branch": w(ks[12], (L, ATTN_WIDTH, D_MODEL), ATTN_WIDTH),
        "w_conv_branch": w(ks[13], (L, CONV_WIDTH, D_MODEL), CONV_WIDTH),
        "w_out": w(ks[14], (L, D_MODEL, D_MODEL), D_MODEL),
        "ffn2_norm_g": gain(ks[15], (L, D_MODEL)),
        "ffn2_w_gate": w(ks[16], (L, D_MODEL, D_FF), D_MODEL),
        "ffn2_w_up": w(ks[17], (L, D_MODEL, D_FF), D_MODEL),
        "ffn2_w_down": w(ks[18], (L, D_FF, D_MODEL), D_FF),
    }


def reference(x, meta_tokens, ffn1_norm_g, ffn1_w_gate, ffn1_w_up, ffn1_w_down,
              mix_norm_g, w_in, q_norm_g, k_norm_g, conv_w, conv_b,
              w_attn_branch, w_conv_branch, w_out,
              ffn2_norm_g, ffn2_w_gate, ffn2_w_up, ffn2_w_down):
    B, S, D = x.shape
    meta = jnp.broadcast_to(meta_tokens[None].astype(x.dtype), (B, N_META, D))
    h = jnp.concatenate([meta, x], axis=1)
    T = h.shape[1]
    k_sel = min(TOPK_MAX, S // 4)
    cos_a, sin_a = rope_tables(T, HEAD_DIM // ROT_DIV)
    cos_i, sin_i = rope_tables(T, IDX_DIM // ROT_DIV)

    for l in range(DEPTH):
        u = rms_norm(h, ffn1_norm_g[l])
        h = h + 0.5 * swiglu(u, ffn1_w_gate[l], ffn1_w_up[l], ffn1_w_down[l])

        u = rms_norm(h, mix_norm_g[l])
        z = u @ w_in[l]
        q, k, v, qi, ki, wi, xc, gate_b, gate_c, ga, gc = jnp.split(z, SPLIT_POINTS, axis=-1)

        q = partial_rope(rms_norm(q.reshape(B, T, N_HEADS, HEAD_DIM), q_norm_g[l]), cos_a, sin_a)
        k = partial_rope(rms_norm(k.reshape(B, T, N_KV_HEADS, HEAD_DIM), k_norm_g[l]), cos_a, sin_a)
        v = v.reshape(B, T, N_KV_HEADS, HEAD_DIM)
        qi = partial_rope(qi.reshape(B, T, N_IDX_HEADS, IDX_DIM), cos_i, sin_i)
        ki = partial_rope(ki.reshape(B, T, 1, IDX_DIM), cos_i, sin_i)[:, :, 0]
        y_attn = dsa_sparse_attention(q, k, v, qi, ki, wi, k_sel) @ w_attn_branch[l]

        y_conv = (gate_b * causal_depthwise_conv(gate_c * xc, conv_w[l], conv_b[l])) @ w_conv_branch[l]

        merged = jax.nn.sigmoid(ga) * y_attn + jax.nn.sigmoid(gc) * y_conv
        h = h + merged @ w_out[l]

        u = rms_norm(h, ffn2_norm_g[l])
        h = h + 0.5 * swiglu(u, ffn2_w_gate[l], ffn2_w_up[l], ffn2_w_down[l])

    return h[:, N_META:]
```

```python
import math
from contextlib import ExitStack

import numpy as np
import concourse.bass as bass
import concourse.mybir as mybir
from concourse.bass_utils import run_bass_kernel_spmd

F32 = mybir.dt.float32
BF16 = mybir.dt.bfloat16
AF = mybir.ActivationFunctionType
ALU = mybir.AluOpType

D = 2048
DFF = 5632
NFF = DFF // 128
TPOS = 2064
NT = 688
NH = 344
NTILE = 3
SUBS = [(0, 128), (128, 128), (256, 88), (344, 128), (472, 128), (600, 88)]
NSUB = 18
NSLOT = 6
EPS = 1e-6
NEG = -30000.0
ENGS = ["pe", "act", "dve", "pool", "sp"]

C_K, C_KI, C_Q, C_QI, C_XC, C_GB, C_GC, C_GA, C_GC2 = 0, 2, 3, 11, 19, 27, 35, 43, 59
N_WIN = 75
CG1, CGM, CG2, CQG, CKG, CCW, CCB = 0, 16, 32, 48, 49, 50, 74
CPW = 82
NCST = 114
NBIS = 26


class Res:
    __slots__ = ("name", "writer", "readers", "ov")

    def __init__(self, name):
        self.name = name
        self.writer = None
        self.readers = {}
        self.ov = [self]


def alias(*rs):
    for a in rs:
        for b in rs:
            if b not in a.ov:
                a.ov.append(b)


class Prog:
    def __init__(self, nc, ctx):
        self.nc = nc
        self.ctx = ctx
        self.ops = {e: [] for e in ENGS}
        self.sems = {}
        self.cnt = {}
        self.waited = {e: {} for e in ENGS}
        for e in ENGS:
            self._sem("E_" + e)

    def _sem(self, key):
        if key not in self.sems:
            self.sems[key] = self.ctx.enter_context(self.nc.semaphore(key))
            self.cnt[key] = 0
        return self.sems[key]

    def _collect(self, eng, reads, writes, extra):
        deps = {}

        def add(d):
            if d is None:
                return
            k, v = d
            if deps.get(k, 0) < v:
                deps[k] = v

        for r in reads:
            for y in r.ov:
                add(y.writer)
        for w in writes:
            for y in w.ov:
                add(y.writer)
                for k, v in y.readers.items():
                    add((k, v))
        for d in extra:
            add(d)
        waits = []
        wd = self.waited[eng]
        for k, v in deps.items():
            if wd.get(k, 0) < v:
                wd[k] = v
                waits.append((self.sems[k], v))
        return waits

    def _commit(self, tick, reads, writes):
        for r in reads:
            if r not in writes:
                k, v = tick
                if r.readers.get(k, 0) < v:
                    r.readers[k] = v
        for w in writes:
            w.writer = tick
            w.readers = {}

    def op(self, eng, fn, reads=(), writes=(), extra=()):
        waits = self._collect(eng, reads, writes, extra)
        key = "E_" + eng
        self.cnt[key] += 1
        tick = (key, self.cnt[key])
        self.ops[eng].append((fn, waits, (self.sems[key], 1)))
        self._commit(tick, reads, writes)
        return tick

    def dma(self, queue, out, in_, sem, reads=(), writes=(), extra=(), **kw):
        self._sem(sem)
        waits = self._collect(queue, reads, writes, extra)
        self.cnt[sem] += 16
        tick = (sem, self.cnt[sem])

        def fn(e, out=out, in_=in_, kw=kw):
            return e.dma_start(out=out, in_=in_, **kw)

        self.ops[queue].append((fn, waits, (self.sems[sem], 16)))
        self._commit(tick, reads, writes)
        return tick

    def wait(self, eng, deps):
        waits = self._collect(eng, (), (), deps)
        self.ops[eng].append((None, waits, None))

    def emit(self):
        nc = self.nc
        with nc.Block() as block:
            def replay(name):
                def run(e):
                    for fn, waits, inc in self.ops[name]:
                        for s, v in waits:
                            e.wait_ge(s, v)
                        if fn is None:
                            continue
                        ins = fn(e)
                        if inc is not None:
                            ins.then_inc(inc[0], inc[1])
                return run
            block.tensor(replay("pe"))
            block.scalar(replay("act"))
            block.vector(replay("dve"))
            block.gpsimd(replay("pool"))
            block.sync(replay("sp"))


def build(n_tiles=NTILE, dbg=None):
    nc = bass.Bass("TRN2", target_bir_lowering=False)
    dram = lambda n, s: nc.dram_tensor(n, s, F32, kind="ExternalInput").ap()
    xT = dram("xT", [128, 16, TPOS])
    WG = [dram("wg1", [NFF, 128, 2048]), dram("wg2", [NFF, 128, 2048])]
    WU = [dram("wu1", [NFF, 128, 2048]), dram("wu2", [NFF, 128, 2048])]
    WD = [dram("wd1", [16, 128, DFF]), dram("wd2", [16, 128, DFF])]
    WIN = dram("win", [N_WIN, 128, 2048])
    WVW = dram("wvw", [3, 128, 6 * 272])
    WAT = dram("wat", [16, 128, 1024])
    WCV = dram("wcv", [16, 128, 1024])
    WOUT = dram("wout", [16, 128, 2048])
    CST = dram("cst", [128, NCST])
    TAB = dram("tab", [4, 128, TPOS])
    PRM = dram("prm", [3, 128, 128])
    outT = nc.dram_tensor("outT", [128, 16, 2048], F32, kind="ExternalOutput").ap()
    dbg_out = None
    if dbg:
        dbg_out = nc.dram_tensor("dbg", [128, dbg], F32, kind="ExternalOutput").ap()

    with ExitStack() as ctx:
        P = Prog(nc, ctx)
        layout = {}
        cur = [0]

        def carve(name, nbytes, at=None):
            nb = (nbytes + 63) // 64 * 64
            lo = cur[0] if at is None else at
            layout[name] = (lo, lo + nb)
            if at is None:
                cur[0] += nb
            return lo

        carve("h", 16 * NT * 4)
        carve("u", 16 * NT * 2)
        for i in range(NSLOT):
            carve(f"ws{i}", 4096)
        carve("kT", 2 * TPOS * 2)
        carve("vtok", NSUB * 256 * 2)
        carve("kiT", TPOS * 2)
        carve("cst", NCST * 4)
        carve("perm", 2 * 128 * 4)
        carve("ident", 128 * 2)
        carve("ones", 128 * 2)
        carve("negc", 128 * 2)
        carve("epsc", 64)
        carve("halo", 8 * 2 * 4)
        carve("wtok", 6 * 16 * 4)
        for i in range(2):
            carve(f"bsc{i}", 64 * 4)
            carve(f"pw{i}", 32 * 4)
        for i in range(7):
            carve(f"T{i}", 696 * 4)
        for i in range(2):
            carve(f"sq{i}", NT * 2)
        for i in range(2):
            carve(f"rl{i}", 512 * 4)
        carve("mb3", TPOS * 2)
        for i in range(4):
            carve(f"pt{i}", NH * 2)
        carve("ycx0", NT * 2); carve("ycx1", NT * 2)
        a_lo = carve("a", NFF * NT * 2)
        off = a_lo
        for nm, nb in (("qT", 8 * NT * 2), ("qiT", 8 * NT * 2), ("yab", 8 * NT * 2),
                       ("isc0", TPOS * 4), ("isc1", TPOS * 4), ("mb0", TPOS * 2), ("mb1", TPOS * 2)):
            carve(nm, nb, at=off)
            off = layout[nm][1]
        assert off <= layout["a"][1], (off, layout["a"])
        layout["merged"] = (layout["qT"][0], layout["qiT"][1])
        layout["mb2"] = (layout["T1"][0], layout["T1"][0] + 4160)
        assert layout["mb2"][1] <= layout["T2"][1]
        layout["junk"] = (layout["T3"][0], layout["T3"][0] + 4160)
        assert layout["junk"][1] <= layout["T4"][1]
        for i in range(3):
            layout[f"Tb{i}"] = (layout["isc0"][0] + i * NT * 4, layout["isc0"][0] + (i + 1) * NT * 4)
        assert layout["Tb2"][1] <= layout["isc0"][1]
        for pre in ("T", "Tb"):
            for i in range(3):
                for hf in range(2):
                    lo = layout[f"{pre}{i}"][0] + hf * NH * 4
                    layout[f"{pre}{i}h{hf}"] = (lo, lo + NH * 4)
        for c in range(8):
            for hf in range(2):
                for nm in ("qT", "qiT"):
                    lo = layout[nm][0] + c * NT * 2 + hf * NH * 2
                    layout[f"{nm}{c}h{hf}"] = (lo, lo + NH * 2)
        for m in range(16):
            for hf in range(2):
                lo = layout["merged"][0] + m * NT * 2 + hf * NH * 2
                layout[f"mg{m}h{hf}"] = (lo, lo + NH * 2)
        for c in range(16):
            layout[f"h{c}"] = (layout["h"][0] + c * NT * 4, layout["h"][0] + (c + 1) * NT * 4)
            layout[f"u{c}"] = (layout["u"][0] + c * NT * 2, layout["u"][0] + (c + 1) * NT * 2)
        for p_ in range(2):
            for hf in range(2):
                lo = layout[f"sq{p_}"][0] + hf * NH * 2
                layout[f"sq{p_}h{hf}"] = (lo, lo + NH * 2)
        homes = [layout["T5"][0], layout["T5"][0] + NT * 2, layout["T6"][0], layout["T6"][0] + NT * 2,
                 layout["sq0"][0], layout["sq1"][0], layout["ycx0"][0], layout["ycx1"][0]]
        for c, lo in enumerate(homes):
            layout[f"ycb{c}"] = (lo, lo + NT * 2)
        total = cur[0]
        assert total <= 212800, total
        arena = ctx.enter_context(nc.sbuf_tensor("arena", [128, total // 2], BF16))

        def vb(name):
            lo, hi = layout[name]
            return arena[:, lo // 2: hi // 2]

        def vf(name):
            lo, hi = layout[name]
            return arena[:, lo // 2: hi // 2].bitcast(F32)

        R = {n: Res(n) for n in layout}
        names = list(layout)
        for i, a_ in enumerate(names):
            for b_ in names[i + 1:]:
                la, lb = layout[a_], layout[b_]
                if la[0] < lb[1] and lb[0] < la[1]:
                    alias(R[a_], R[b_])

        h = vf("h"); u = vb("u"); a = vb("a")
        ws = [vb(f"ws{i}") for i in range(NSLOT)]
        kT = vb("kT"); vtok = vb("vtok"); kiT = vb("kiT")
        cst = vf("cst"); perm = vf("perm"); ident = vb("ident"); ones = vb("ones"); negc = vb("negc")
        epsc = vf("epsc"); halo = vf("halo"); wtok = vf("wtok")
        bsc = [vf("bsc0"), vf("bsc1")]; pw = [vf("pw0"), vf("pw1")]
        Rbsc = [R["bsc0"], R["bsc1"]]; Rpw = [R["pw0"], R["pw1"]]
        T = [vf(f"T{i}") for i in range(7)]
        sq = [vb(f"sq{i}") for i in range(2)]
        rl = [vf(f"rl{i}") for i in range(2)]
        pt = [vb(f"pt{i}") for i in range(4)]
        qT = vb("qT"); qiT = vb("qiT"); yab = vb("yab")
        iscs = [vf("isc0"), vf("isc1")]; Risc = [R["isc0"], R["isc1"]]
        junk = vb("junk")
        mb = [vb(f"mb{i}") for i in range(4)]
        MBI = [0, 1, 2, 3, 0, 1]
        merged = vb("merged")
        ycbv = [vb(f"ycb{c}") for c in range(8)]
        Rycb = [R[f"ycb{c}"] for c in range(8)]
        TT = [[T[0], T[1], T[2]], [vf("Tb0"), vf("Tb1"), vf("Tb2")]]
        TPRE = ["T", "Tb"]
        RT = [R[f"T{i}"] for i in range(7)]
        Rsq = [R[f"sq{i}"] for i in range(2)]
        Rrl = [R[f"rl{i}"] for i in range(2)]
        Rpt = [R[f"pt{i}"] for i in range(4)]
        Rws = [R[f"ws{i}"] for i in range(NSLOT)]
        Rmb = [R[f"mb{i}"] for i in range(4)]

        Rh = [R[f"h{c}"] for c in range(16)]
        Ru = [R[f"u{c}"] for c in range(16)]
        ps = [ctx.enter_context(nc.psum_tensor(f"ps{i}", [128, 512], F32)) for i in range(8)]
        Rps = [Res(f"ps{i}") for i in range(8)]

        dbg_col = [0]

        def dump(ap, res, ncols):
            if dbg_out is None:
                return
            c0 = dbg_col[0]
            np_ = ap.shape[0]
            P.dma("pool", dbg_out[0:np_, c0:c0 + ncols], ap, "st_dbg", reads=[res])
            dbg_col[0] += ncols
            return c0

        wq = []
        wstate = {"issued": 0, "used": 0, "rel": 0}

        def wplan(ap, ncols):
            wq.append((ap, ncols))

        def wissue_upto(k):
            while wstate["issued"] < min(k, len(wq)):
                i = wstate["issued"]
                ap, ncols = wq[i]
                s = i % NSLOT
                P.dma("pool", ws[s][:, 0:ncols], ap, f"ld_w{s}", writes=[Rws[s]], max_dma_last_dim=4096)
                wstate["issued"] += 1

        def wnext():
            i = wstate["used"]
            wissue_upto(wstate["rel"] + NSLOT)
            assert wstate["issued"] > i, "weight ring over-subscribed"
            wstate["used"] += 1
            return i % NSLOT

        def wrel(n):
            wstate["rel"] += n
            assert wstate["rel"] <= wstate["used"]
            wissue_upto(wstate["rel"] + NSLOT)

        P.dma("sp", cst[:, 0:NCST], CST[:, :], "ld_c0", writes=[R["cst"]])
        P.dma("sp", perm[:, 0:128], PRM[0], "ld_c1", writes=[R["perm"]])
        P.dma("sp", perm[:, 128:256], PRM[1], "ld_c2", writes=[R["perm"]])
        P.dma("pool", ident[:, :], PRM[2], "ld_id", writes=[R["ident"]])
        P.op("pool", lambda e: e.memset(ones[:, :], 1.0), writes=[R["ones"]])
        P.op("pool", lambda e: e.memset(negc[:, :], NEG), writes=[R["negc"]])
        P.op("pool", lambda e: e.memset(epsc[:, :], EPS), writes=[R["epsc"]])
        P.op("pool", lambda e: e.memset(halo[:, :], 0.0), writes=[R["halo"]])

        def mm_group(bank, M, N, pairs, reads, col0=0):
            def fn(e, bank=bank, M=M, N=N, pairs=pairs, col0=col0):
                ins = None
                n = len(pairs)
                for i, (l, r) in enumerate(pairs):
                    ins = e.matmul(ps[bank][0:M, col0:col0 + N], lhsT=l, rhs=r, start=(i == 0), stop=(i == n - 1))
                return ins
            return P.op("pe", fn, reads=reads, writes=[Rps[bank]])

        def rmsnorm(gcol):
            for c in range(16):
                P.op("act", lambda e, c=c: e.activation(out=sq[c % 2][:, 0:NT], in_=h[:, c * NT:(c + 1) * NT], func=AF.Square),
                     reads=[Rh[c]], writes=[Rsq[c % 2]])
                for hf in range(2):
                    def fn(e, c=c, hf=hf):
                        return e.matmul(ps[hf][:, 0:NH], lhsT=ones[:, :], rhs=sq[c % 2][:, hf * NH:(hf + 1) * NH],
                                        start=(c == 0), stop=(c == 15))
                    P.op("pe", fn, reads=[Rsq[c % 2], R["ones"]], writes=[Rps[hf]])
            for hf in range(2):
                P.op("act", lambda e, hf=hf: e.activation(out=T[2][:, hf * NH:(hf + 1) * NH], in_=ps[hf][:, 0:NH], func=AF.Ln,
                                                          bias=epsc[:, 0:1], scale=1.0 / D),
                     reads=[Rps[hf], R["epsc"]], writes=[RT[2]])
            P.op("act", lambda e: e.activation(out=T[2][:, 0:NT], in_=T[2][:, 0:NT], func=AF.Exp, scale=-0.5), reads=[RT[2]], writes=[RT[2]])
            for c in range(16):
                P.op("dve", lambda e, c=c: e.scalar_tensor_tensor(out=u[:, c * NT:(c + 1) * NT], in0=h[:, c * NT:(c + 1) * NT],
                                                                  scalar=cst[:, gcol + c:gcol + c + 1], in1=T[2][:, 0:NT],
                                                                  op0=ALU.mult, op1=ALU.mult),
                     reads=[Rh[c], RT[2], R["cst"]], writes=[Ru[c]])

        def ffn(which, gcol, store_tile=None):
            rmsnorm(gcol)
            for j in range(NFF):
                sg_ = wnext(); su_ = wnext()
                b0 = (j % 2) * 4
                for (s_, bb) in ((sg_, b0), (su_, b0 + 2)):
                    for hf in range(2):
                        pairs = [(ws[s_][:, k * 128:(k + 1) * 128], u[:, k * NT + hf * NH: k * NT + (hf + 1) * NH]) for k in range(16)]
                        if j == 0 and s_ == sg_ and hf == 0:
                            for k in range(16):
                                P.op("pe", lambda e, k=k, bank=bb + hf, l=pairs[k][0], r=pairs[k][1]: e.matmul(
                                    ps[bank][:, 0:NH], lhsT=l, rhs=r, start=(k == 0), stop=(k == 15)),
                                    reads=[Rws[s_], Ru[k]], writes=[Rps[bb + hf]])
                        else:
                            mm_group(bb + hf, 128, NH, pairs, [Rws[s_], R["u"]])
                wrel(2)
                q_ = j % 2
                for hf in range(2):
                    P.op("act", lambda e, q_=q_, hf=hf, b=b0 + hf: e.activation(out=T[q_][:, hf * NH:(hf + 1) * NH], in_=ps[b][:, 0:NH], func=AF.Silu),
                         reads=[Rps[b0 + hf]], writes=[RT[q_]])
                    P.op("dve", lambda e, q_=q_, hf=hf, b=b0 + 2 + hf, j=j: e.tensor_tensor(
                        out=a[:, j * NT + hf * NH: j * NT + (hf + 1) * NH], in0=ps[b][:, 0:NH], in1=T[q_][:, hf * NH:(hf + 1) * NH], op=ALU.mult),
                        reads=[Rps[b0 + 2 + hf], RT[q_]], writes=[R["a"]])
            for m in range(16):
                sl = [wnext() for _ in range(3)]
                for hf in range(2):
                    bank = (m % 4) * 2 + hf
                    pairs = [(ws[sl[hc // 16]][:, (hc % 16) * 128:(hc % 16 + 1) * 128], a[:, hc * NT + hf * NH: hc * NT + (hf + 1) * NH])
                             for hc in range(NFF)]
                    mm_group(bank, 128, NH, pairs, [Rws[s] for s in sl] + [R["a"]])
                    if hf == 1:
                        wrel(3)
                    P.op("dve", lambda e, m=m, hf=hf, bank=bank: e.scalar_tensor_tensor(
                        out=h[:, m * NT + hf * NH: m * NT + (hf + 1) * NH], in0=ps[bank][:, 0:NH], scalar=0.5,
                        in1=h[:, m * NT + hf * NH: m * NT + (hf + 1) * NH], op0=ALU.mult, op1=ALU.add),
                        reads=[Rps[bank], Rh[m]], writes=[Rh[m]])
                if store_tile is not None:
                    p0_ = store_tile * NT
                    if store_tile == 0:
                        P.dma("sp", outT[:, m, 0:NT - 16], h[:, m * NT + 16:(m + 1) * NT], f"st_o{m}", reads=[Rh[m]])
                    else:
                        P.dma("sp", outT[:, m, p0_ - 16:p0_ - 16 + NT], h[:, m * NT:(m + 1) * NT], f"st_o{m}", reads=[Rh[m]])

        def plan_ffn(which):
            for j in range(NFF):
                wplan(WG[which][j], 2048); wplan(WU[which][j], 2048)
            for m in range(16):
                wplan(WD[which][m][:, 0:2048], 2048); wplan(WD[which][m][:, 2048:4096], 2048); wplan(WD[which][m][:, 4096:DFF], 1536)

        def plan_mixer():
            for i in range(3):
                wplan(WVW[i], 6 * 272)
            for c in (C_K, C_K + 1, C_KI):
                wplan(WIN[c], 2048)
            for c in range(8):
                wplan(WIN[C_Q + c], 2048)
            for c in range(8):
                wplan(WIN[C_QI + c], 2048)
            for c in range(8):
                wplan(WIN[C_XC + c], 2048); wplan(WIN[C_GC + c], 2048); wplan(WIN[C_GB + c], 2048)
            for hf in range(2):
                for m in range(16):
                    wplan(WIN[C_GA + m], 2048); wplan(WIN[C_GC2 + m], 2048); wplan(WAT[m], 1024); wplan(WCV[m], 1024)
                for m in range(16):
                    wplan(WOUT[m], 2048)

        for it in range(n_tiles):
            plan_ffn(0); plan_mixer(); plan_ffn(1)

        def proj_chunk(slot, bank0):
            for hf in range(2):
                pairs = [(ws[slot][:, k * 128:(k + 1) * 128], u[:, k * NT + hf * NH: k * NT + (hf + 1) * NH]) for k in range(16)]
                mm_group(bank0 + hf, 128, NH, pairs, [Rws[slot], R["u"]])
            wrel(1)

        SHUF = [list(range(16, 32)) + list(range(0, 16)), list(range(8, 16)) + list(range(0, 8)) + list(range(16, 32))]

        def rope_chunk(bank0, do_norm, gcol, tabi, permi, dst, dst_res, dst_col, par):
            A, B, C = TT[par]
            for hf in range(2):
                RA, RB, RC = [R[f"{TPRE[par]}{i}h{hf}"] for i in range(3)]
                Rq = R[f"sq{par}h{hf}"]
                sqv = sq[par]
                b = bank0 + hf
                b2 = bank0 + 2 + hf
                hs = slice(hf * NH, (hf + 1) * NH)
                if do_norm:
                    P.op("act", lambda e, hs=hs, b=b: e.activation(out=sqv[:, hs], in_=ps[b][:, 0:NH], func=AF.Square),
                         reads=[Rps[b]], writes=[Rq])
                    P.op("pe", lambda e, hs=hs, b2=b2: e.matmul(ps[b2][:, 0:NH], lhsT=ones[:, :], rhs=sqv[:, hs], start=True, stop=True),
                         reads=[Rq, R["ones"]], writes=[Rps[b2]])
                    P.op("act", lambda e, hs=hs, b2=b2: e.activation(out=A[:, hs], in_=ps[b2][:, 0:NH], func=AF.Ln, bias=epsc[:, 0:1], scale=1.0 / 128),
                         reads=[Rps[b2], R["epsc"]], writes=[RA])
                    P.op("act", lambda e, hs=hs: e.activation(out=A[:, hs], in_=A[:, hs], func=AF.Exp, scale=-0.5), reads=[RA], writes=[RA])
                    P.op("dve", lambda e, hs=hs, b=b: e.scalar_tensor_tensor(out=B[:, hs], in0=ps[b][:, 0:NH], scalar=cst[:, gcol:gcol + 1],
                                                                            in1=A[:, hs], op0=ALU.mult, op1=ALU.mult),
                         reads=[Rps[b], RA, R["cst"]], writes=[RB])
                    P.op("dve", lambda e, hs=hs: e.stream_shuffle(out=C[:, hs], in_=B[:, hs], mask=SHUF[permi]), reads=[RB], writes=[RC])
                    P.op("dve", lambda e, hs=hs: e.tensor_tensor(out=B[:, hs], in0=B[:, hs], in1=T[3 + 2 * tabi][:, hs], op=ALU.mult),
                         reads=[RB, RT[3 + 2 * tabi]], writes=[RB])
                else:
                    P.op("act", lambda e, hs=hs, b=b: e.activation(out=C[:, hs], in_=ps[b][:, 0:NH], func=AF.Copy), reads=[Rps[b]], writes=[RC])
                    P.op("dve", lambda e, hs=hs: e.stream_shuffle(out=C[:, hs], in_=C[:, hs], mask=SHUF[permi]), reads=[RC], writes=[RC])
                    P.op("dve", lambda e, hs=hs, b=b: e.tensor_tensor(out=B[:, hs], in0=ps[b][:, 0:NH], in1=T[3 + 2 * tabi][:, hs], op=ALU.mult),
                         reads=[Rps[b], RT[3 + 2 * tabi]], writes=[RB])
                P.op("dve", lambda e, hs=hs: e.tensor_tensor(out=C[:, hs], in0=C[:, hs], in1=T[4 + 2 * tabi][:, hs], op=ALU.mult),
                     reads=[RC, RT[4 + 2 * tabi]], writes=[RC])
                P.op("dve", lambda e, hs=hs, hf=hf: e.tensor_tensor(out=dst[:, dst_col + hf * NH: dst_col + (hf + 1) * NH], in0=C[:, hs], in1=B[:, hs], op=ALU.add),
                     reads=[RB, RC], writes=[dst_res[hf] if isinstance(dst_res, list) else dst_res])

        rot = {"sbank": 0, "grp": 0}

        def idx_scores(it, j, q, banks=(4, 5, 6, 7)):
            p0 = it * NT
            o, S = SUBS[j]
            L = p0 + o + S
            isc = iscs[q]
            nblk = (L + 511) // 512
            for blk in range(nblk):
                c0 = blk * 512
                W = min(512, L - c0)
                for hd in range(16):
                    bank = banks[rot["sbank"] % len(banks)]
                    r_ = rot["sbank"] % 2
                    rot["sbank"] += 1
                    pr = slice((hd % 2) * 64, (hd % 2) * 64 + 64)
                    qc = (hd // 2) * NT + o
                    P.op("pe", lambda e, bank=bank, S=S, W=W, pr=pr, qc=qc, c0=c0: e.matmul(
                        ps[bank][0:S, 0:W], lhsT=qiT[pr, qc:qc + S], rhs=kiT[pr, c0:c0 + W], start=True, stop=True),
                        reads=[R[f"qiT{hd // 2}h{j // 3}"], R["kiT"]], writes=[Rps[bank]])
                    P.op("act", lambda e, bank=bank, r_=r_, S=S, W=W: e.activation(out=rl[r_][0:S, 0:W], in_=ps[bank][0:S, 0:W], func=AF.Relu),
                         reads=[Rps[bank]], writes=[Rrl[r_]])
                    wc = j * 16 + hd
                    if hd == 0:
                        P.op("dve", lambda e, r_=r_, S=S, W=W, c0=c0, wc=wc: e.tensor_scalar(
                            out=isc[0:S, c0:c0 + W], in0=rl[r_][0:S, 0:W], scalar1=wtok[0:S, wc:wc + 1], scalar2=None, op0=ALU.mult),
                            reads=[Rrl[r_], R["wtok"]], writes=[Risc[q]])
                    else:
                        P.op("dve", lambda e, r_=r_, S=S, W=W, c0=c0, wc=wc: e.scalar_tensor_tensor(
                            out=isc[0:S, c0:c0 + W], in0=rl[r_][0:S, 0:W], scalar=wtok[0:S, wc:wc + 1],
                            in1=isc[0:S, c0:c0 + W], op0=ALU.mult, op1=ALU.add),
                            reads=[Rrl[r_], R["wtok"], Risc[q]], writes=[Risc[q]])
                yield

        def bis_setup(it, j, q):
            p0 = it * NT
            o, S = SUBS[j]
            L = p0 + o + S
            isc = iscs[q]; b = bsc[q]; Rb = Rbsc[q]
            P.op("dve", lambda e: e.max(out=b[0:S, 16:24], in_=isc[0:S, 0:L]), reads=[Risc[q]], writes=[Rb])
            P.op("act", lambda e: e.activation(out=junk[0:S, 0:L], in_=isc[0:S, 0:L], func=AF.Copy, scale=-1.0), reads=[Risc[q]], writes=[R["junk"]])
            P.op("dve", lambda e: e.max(out=b[0:S, 8:16], in_=junk[0:S, 0:L]), reads=[R["junk"]], writes=[Rb])
            P.op("pool", lambda e: e.affine_select(out=isc[0:S, L - S:L], in_=isc[0:S, L - S:L], pattern=[[-1, S]],
                                                   compare_op=ALU.is_ge, fill=-1.0e30, base=0, channel_multiplier=1),
                 reads=[Risc[q]], writes=[Risc[q]])
            P.op("dve", lambda e: e.tensor_scalar(out=b[0:S, 5:6], in0=b[0:S, 8:9], scalar1=1.0 + 2.0 ** -7, scalar2=None, op0=ALU.mult), reads=[Rb], writes=[Rb])
            P.op("dve", lambda e: e.scalar_tensor_tensor(out=b[0:S, 4:5], in0=b[0:S, 8:9], scalar=1.0 - 2.0 ** -7, in1=b[0:S, 5:6], op0=ALU.mult, op1=ALU.max),
                 reads=[Rb], writes=[Rb])
            P.op("dve", lambda e: e.tensor_tensor(out=b[0:S, 6:7], in0=b[0:S, 16:17], in1=b[0:S, 4:5], op=ALU.add), reads=[Rb], writes=[Rb])
            P.op("dve", lambda e: e.tensor_tensor(out=b[0:S, 7:8], in0=b[0:S, 16:17], in1=b[0:S, 4:5], op=ALU.subtract), reads=[Rb], writes=[Rb])
            P.op("dve", lambda e: e.tensor_scalar(out=b[0:S, 1:2], in0=b[0:S, 7:8], scalar1=0.5, scalar2=None, op0=ALU.mult), reads=[Rb], writes=[Rb])
            P.op("dve", lambda e: e.tensor_scalar(out=pw[q][0:S, 0:32], in0=cst[0:S, CPW:CPW + 32], scalar1=b[0:S, 6:7], scalar2=None, op0=ALU.mult),
                 reads=[Rb, R["cst"]], writes=[Rpw[q]])

        Rjunkd = Res("junkd")

        def bis_iter(it, j, q, i):
            p0 = it * NT
            o, S = SUBS[j]
            L = p0 + o + S
            isc = iscs[q]; b = bsc[q]; Rb = Rbsc[q]
            if q == 1 or i % 2 == 1:
                P.op("act", lambda e: e.activation(out=junk[0:S, 0:L], in_=isc[0:S, 0:L], func=AF.Sign, bias=b[0:S, 1:2], scale=-1.0, accum_out=b[0:S, 0:1]),
                     reads=[Risc[q], Rb], writes=[R["junk"], Rb])
                P.op("dve", lambda e: e.tensor_scalar(out=b[0:S, 2:3], in0=b[0:S, 0:1], scalar1=float(L - 511), scalar2=0.5, op0=ALU.is_le, op1=ALU.subtract),
                     reads=[Rb], writes=[Rb])
            else:
                P.op("dve", lambda e: e.tensor_scalar(out=junk[0:S, 0:L], in0=isc[0:S, 0:L], scalar1=b[0:S, 1:2], scalar2=0.0, op0=ALU.is_ge, op1=ALU.add,
                                                      accum_out=b[0:S, 0:1]),
                     reads=[Risc[q], Rb], writes=[Rjunkd, Rb])
                P.op("dve", lambda e: e.tensor_scalar(out=b[0:S, 2:3], in0=b[0:S, 0:1], scalar1=256.0, scalar2=0.5, op0=ALU.is_ge, op1=ALU.subtract),
                     reads=[Rb], writes=[Rb])
            P.op("dve", lambda e: e.scalar_tensor_tensor(out=b[0:S, 1:2], in0=b[0:S, 2:3], scalar=pw[q][0:S, i:i + 1], in1=b[0:S, 1:2], op0=ALU.mult, op1=ALU.add),
                 reads=[Rb, Rpw[q]], writes=[Rb])

        def bis_final(it, j, q):
            p0 = it * NT
            o, S = SUBS[j]
            L = p0 + o + S
            jj = MBI[j]
            if L > 256:
                isc = iscs[q]; b = bsc[q]; Rb = Rbsc[q]
                P.op("dve", lambda e: e.scalar_tensor_tensor(out=b[0:S, 3:4], in0=pw[q][0:S, NBIS:NBIS + 1], scalar=-1.0, in1=b[0:S, 1:2], op0=ALU.mult, op1=ALU.add),
                     reads=[Rb, Rpw[q]], writes=[Rb])
                P.op("dve", lambda e: e.tensor_scalar(out=mb[jj][0:S, 0:L], in0=isc[0:S, 0:L], scalar1=b[0:S, 3:4], scalar2=NEG, op0=ALU.is_lt, op1=ALU.mult),
                     reads=[Risc[q], Rb], writes=[Rmb[jj]])
            else:
                P.op("dve", lambda e: e.memset(mb[jj][0:S, 0:L], 0.0), writes=[Rmb[jj]])
            P.op("pool", lambda e: e.affine_select(out=mb[jj][0:S, L - S:L], in_=mb[jj][0:S, L - S:L], pattern=[[-1, S]],
                                                   compare_op=ALU.is_ge, fill=NEG, base=0, channel_multiplier=1),
                 reads=[Rmb[jj]], writes=[Rmb[jj]])

        def needs_bis(it, j):
            o, S = SUBS[j]
            return it * NT + o + S > 256

        def attn_group(it, hf, hd, sbanks=(4, 5, 6, 7), LA=2):
            nch = it * 6 + hf * 3 + 3
            gq = hd // 4
            ob = rot["grp"] % 2
            sb_ = 2 + rot["grp"] % 2
            rot["grp"] += 1
            q0 = hd * NT + hf * NH
            units = []
            for c in range(nch):
                ti, jc = divmod(c, 6)
                oc, Sc = SUBS[jc]
                units.append((c, ti * NT + oc, Sc))
            nu = len(units)

            def emit_qk(c, pc, Sc, k):
                bank = sbanks[k % len(sbanks)]

                def fn(e):
                    e.matmul(ps[bank][0:Sc, 0:NH], lhsT=kT[:, gq * TPOS + pc: gq * TPOS + pc + Sc], rhs=qT[:, q0:q0 + NH], start=True, stop=False)
                    ins = None
                    for jj in range(3):
                        o_, S_ = SUBS[hf * 3 + jj]
                        cs = o_ - hf * NH
                        gt = it * 6 + hf * 3 + jj
                        if c <= gt:
                            l = mb[MBI[hf * 3 + jj]][0:S_, pc:pc + Sc]
                        else:
                            l = negc[0:S_, 0:Sc]
                        ins = e.matmul(ps[bank][0:Sc, cs:cs + S_], lhsT=l, rhs=ident[0:S_, 0:S_], start=False, stop=(jj == 2))
                    return ins
                P.op("pe", fn, reads=[R["kT"], R[f"qT{hd}h{hf}"], R["ident"], R["negc"]] + Rmb, writes=[Rps[bank]])
                P.op("act", lambda e: e.activation(out=pt[k % 4][0:Sc, 0:NH], in_=ps[bank][0:Sc, 0:NH], func=AF.Exp, scale=128.0 ** -0.5),
                     reads=[Rps[bank]], writes=[Rpt[k % 4]])

            def emit_pv(c, pc, Sc, k):
                first = (k == 0)
                last = (k == nu - 1)

                def fn(e):
                    e.matmul(ps[ob][:, 0:NH], lhsT=vtok[0:Sc, c * 256 + gq * 128: c * 256 + gq * 128 + 128], rhs=pt[k % 4][0:Sc, 0:NH], start=first, stop=last)
                    return e.matmul(ps[sb_][:, 0:NH], lhsT=ones[0:Sc, :], rhs=pt[k % 4][0:Sc, 0:NH], start=first, stop=last)
                P.op("pe", fn, reads=[R["vtok"], Rpt[k % 4], R["ones"]], writes=[Rps[ob], Rps[sb_]])

            for k in range(nu + LA):
                if k < nu:
                    emit_qk(*units[k], k)
                if k >= LA:
                    emit_pv(*units[k - LA], k - LA)
            P.op("act", lambda e: e.activation(out=T[0][:, 0:NH], in_=ps[sb_][:, 0:NH], func=AF.Ln), reads=[Rps[sb_]], writes=[RT[0]])
            P.op("act", lambda e: e.activation(out=T[0][:, 0:NH], in_=T[0][:, 0:NH], func=AF.Exp, scale=-1.0), reads=[RT[0]], writes=[RT[0]])
            P.op("dve", lambda e: e.tensor_tensor(out=yab[:, q0:q0 + NH], in0=ps[ob][:, 0:NH], in1=T[0][:, 0:NH], op=ALU.mult),
                 reads=[Rps[ob], RT[0]], writes=[R["yab"]])

        def conv_chunk(c):
            sx = wnext(); sc_ = wnext(); sg_ = wnext()
            proj_chunk(sx, 0); proj_chunk(sc_, 2)
            for hf in range(2):
                P.op("act", lambda e, hf=hf: e.activation(out=T[0][:, hf * NH:(hf + 1) * NH], in_=ps[hf][:, 0:NH], func=AF.Copy),
                     reads=[Rps[hf]], writes=[RT[0]])
            proj_chunk(sg_, 0)
            P.op("dve", lambda e: e.tensor_copy(out=T[1][:, 0:2], in_=halo[:, c * 2:c * 2 + 2]), reads=[R["halo"]], writes=[RT[1]])
            for hf in range(2):
                P.op("dve", lambda e, hf=hf: e.tensor_tensor(out=T[1][:, 2 + hf * NH: 2 + (hf + 1) * NH], in0=ps[2 + hf][:, 0:NH],
                                                             in1=T[0][:, hf * NH:(hf + 1) * NH], op=ALU.mult),
                     reads=[Rps[2 + hf], RT[0]], writes=[RT[1]])
            P.op("act", lambda e: e.activation(out=halo[:, c * 2:c * 2 + 2], in_=T[1][:, NT:NT + 2], func=AF.Copy),
                 reads=[RT[1]], writes=[R["halo"]])
            P.op("act", lambda e: e.activation(out=T[2][:, 0:NT], in_=T[1][:, 2:2 + NT], func=AF.Identity,
                                               bias=cst[:, CCB + c:CCB + c + 1], scale=cst[:, CCW + c * 3 + 2:CCW + c * 3 + 3]),
                 reads=[RT[1], R["cst"]], writes=[RT[2]])
            for tap in (1, 0):
                P.op("dve", lambda e, tap=tap: e.scalar_tensor_tensor(out=T[2][:, 0:NT], in0=T[1][:, tap:tap + NT],
                                                                      scalar=cst[:, CCW + c * 3 + tap:CCW + c * 3 + tap + 1],
                                                                      in1=T[2][:, 0:NT], op0=ALU.mult, op1=ALU.add),
                     reads=[RT[1], RT[2], R["cst"]], writes=[RT[2]])
            for hf in range(2):
                P.op("dve", lambda e, hf=hf: e.tensor_tensor(out=ycbv[c][:, hf * NH:(hf + 1) * NH], in0=ps[hf][:, 0:NH],
                                                             in1=T[2][:, hf * NH:(hf + 1) * NH], op=ALU.mult),
                     reads=[Rps[hf], RT[2]], writes=[Rycb[c]])

        def merge_m(m, hf, b0):
            hs = slice(hf * NH, (hf + 1) * NH)
            for bb in (b0, b0 + 1):
                s_ = wnext()
                pairs = [(ws[s_][:, k * 128:(k + 1) * 128], u[:, k * NT + hf * NH: k * NT + (hf + 1) * NH]) for k in range(16)]
                mm_group(bb, 128, NH, pairs, [Rws[s_], R["u"]])
                wrel(1)
            sat = wnext()
            pairs = [(ws[sat][:, k * 128:(k + 1) * 128], yab[:, k * NT + hf * NH: k * NT + (hf + 1) * NH]) for k in range(8)]
            mm_group(b0 + 2, 128, NH, pairs, [Rws[sat], R["yab"]])
            wrel(1)
            scv = wnext()
            pairs = [(ws[scv][:, k * 128:(k + 1) * 128], ycbv[k][:, hf * NH:(hf + 1) * NH]) for k in range(8)]
            mm_group(b0 + 3, 128, NH, pairs, [Rws[scv]] + Rycb)
            wrel(1)
            P.op("act", lambda e: e.activation(out=T[0][:, hs], in_=ps[b0][:, 0:NH], func=AF.Sigmoid), reads=[Rps[b0]], writes=[RT[0]])
            P.op("act", lambda e: e.activation(out=T[1][:, hs], in_=ps[b0 + 1][:, 0:NH], func=AF.Sigmoid), reads=[Rps[b0 + 1]], writes=[RT[1]])
            P.op("dve", lambda e: e.tensor_tensor(out=T[0][:, hs], in0=ps[b0 + 2][:, 0:NH], in1=T[0][:, hs], op=ALU.mult),
                 reads=[Rps[b0 + 2], RT[0]], writes=[RT[0]])
            P.op("dve", lambda e: e.tensor_tensor(out=T[1][:, hs], in0=ps[b0 + 3][:, 0:NH], in1=T[1][:, hs], op=ALU.mult),
                 reads=[Rps[b0 + 3], RT[1]], writes=[RT[1]])
            P.op("dve", lambda e: e.tensor_tensor(out=merged[:, m * NT + hf * NH: m * NT + (hf + 1) * NH], in0=T[0][:, hs], in1=T[1][:, hs], op=ALU.add),
                 reads=[RT[0], RT[1]], writes=[R[f"mg{m}h{hf}"]])

        def wout_m(m, hf, bank):
            s_ = wnext()
            pairs = [(ws[s_][:, k * 128:(k + 1) * 128], merged[:, k * NT + hf * NH: k * NT + (hf + 1) * NH]) for k in range(16)]
            mm_group(bank, 128, NH, pairs, [Rws[s_]] + [R[f"mg{k}h{hf}"] for k in range(16)])
            wrel(1)
            P.op("dve", lambda e: e.tensor_tensor(out=h[:, m * NT + hf * NH: m * NT + (hf + 1) * NH], in0=ps[bank][:, 0:NH],
                                                  in1=h[:, m * NT + hf * NH: m * NT + (hf + 1) * NH], op=ALU.add),
                 reads=[Rps[bank], Rh[m]], writes=[Rh[m]])

        def mixer(it):
            p0 = it * NT
            rmsnorm(CGM)
            for i in range(4):
                P.dma("sp", T[3 + i][:, 0:NT], TAB[i][:, p0:p0 + NT], f"ld_t{i}", writes=[RT[3 + i]])
            sl = [wnext() for _ in range(3)]
            for j, (o, S) in enumerate(SUBS):
                g = it * 6 + j
                bank = j % 4
                pairs = [(u[:, k * NT + o: k * NT + o + S], ws[sl[k // 6]][:, (k % 6) * 272:(k % 6 + 1) * 272]) for k in range(16)]
                if j == 0:
                    for k in range(16):
                        P.op("pe", lambda e, k=k, bank=bank, S=S, l=pairs[k][0], r=pairs[k][1]: e.matmul(
                            ps[bank][0:S, 0:272], lhsT=l, rhs=r, start=(k == 0), stop=(k == 15)),
                            reads=[Rws[sl[k // 6]], Ru[k]], writes=[Rps[bank]])
                else:
                    mm_group(bank, S, 272, pairs, [Rws[s] for s in sl] + [R["u"]])
                P.op("act", lambda e, S=S, g=g, bank=bank: e.activation(out=vtok[0:S, g * 256:(g + 1) * 256], in_=ps[bank][0:S, 0:256], func=AF.Copy),
                     reads=[Rps[bank]], writes=[R["vtok"]])
                P.op("dve", lambda e, S=S, j=j, bank=bank: e.tensor_copy(out=wtok[0:S, j * 16:(j + 1) * 16], in_=ps[bank][0:S, 256:272]),
                     reads=[Rps[bank]], writes=[R["wtok"]])
            wrel(3)
            chunks = [(True, CKG, 0, 0, kT, R["kT"], c * TPOS + p0) for c in range(2)]
            chunks.append((False, 0, 1, 1, kiT, R["kiT"], p0))
            chunks += [(True, CQG, 0, 0, qT, [R[f"qT{c}h0"], R[f"qT{c}h1"]], c * NT) for c in range(8)]
            chunks += [(False, 0, 1, 1, qiT, [R[f"qiT{c}h0"], R[f"qiT{c}h1"]], c * NT) for c in range(8)]
            proj_chunk(wnext(), 0)
            for i, ch in enumerate(chunks):
                if i + 1 < len(chunks):
                    proj_chunk(wnext(), ((i + 1) % 2) * 4)
                rope_chunk((i % 2) * 4, *ch, i % 2)

            conv_left = list(range(8))
            def est_steps(pi):
                act_ = [2 * pi + q for q in range(2) if needs_bis(it, 2 * pi + q)]
                if not act_:
                    return 1
                return sum((it * NT + SUBS[j][0] + SUBS[j][1] + 511) // 512 for j in act_) + 1 + NBIS
            interval = max(4, (est_steps(0) + est_steps(1)) // 8)
            rnd = {"n": 0}

            def pair_steps(pi, banks):
                act = [(2 * pi + q, q) for q in range(2) if needs_bis(it, 2 * pi + q)]
                for j, q in act:
                    yield from idx_scores(it, j, q, banks)
                for j, q in act:
                    bis_setup(it, j, q)
                yield
                if act:
                    for i in range(NBIS):
                        for j, q in act:
                            bis_iter(it, j, q, i)
                        yield

            for pi in range(2):
                for _ in pair_steps(pi, (4, 5, 6, 7)):
                    if conv_left:
                        rnd["n"] += 1
                        if rnd["n"] % interval == 0:
                            conv_chunk(conv_left.pop(0))
                if pi == 1:
                    while conv_left:
                        conv_chunk(conv_left.pop(0))
                bis_final(it, 2 * pi, 0); bis_final(it, 2 * pi + 1, 1)
            for hd in range(8):
                attn_group(it, 0, hd)
            def half0_steps():
                for m in range(16):
                    merge_m(m, 0, 0)
                    yield
                for m in range(16):
                    wout_m(m, 0, m % 4)
                    yield
            ga_, gb_ = pair_steps(2, (4, 5, 6, 7)), half0_steps()
            da = db = False
            while not (da and db):
                if not da:
                    try:
                        next(ga_)
                    except StopIteration:
                        da = True
                if not db:
                    try:
                        next(gb_)
                    except StopIteration:
                        db = True
            bis_final(it, 4, 0); bis_final(it, 5, 1)
            for hd in range(8):
                attn_group(it, 1, hd)
            if dbg and it == 0:
                dump(yab[:, 0:NT], R["yab"], NT)
            for m in range(16):
                merge_m(m, 1, (m % 2) * 4)
            for m in range(16):
                wout_m(m, 1, m % 8)

        h3 = h.rearrange("p (c t) -> p c t", c=16)
        for it in range(n_tiles):
            p0 = it * NT
            for c in range(16):
                P.dma("sp", h[:, c * NT:(c + 1) * NT], xT[:, c, p0:p0 + NT], f"ld_x{c}", writes=[Rh[c]])
            ffn(0, CG1)
            if dbg and it == 0:
                dump(h[:, 0:NT], R["h"], NT)
            mixer(it)
            if dbg and it == 0:
                dump(h[:, 0:NT], R["h"], NT)
            ffn(1, CG2, store_tile=it)
        fin = [(f"st_o{c}", P.cnt[f"st_o{c}"]) for c in range(16)]
        if dbg:
            fin.append(("st_dbg", P.cnt.get("st_dbg", 0)))
        P.wait("sp", [d for d in fin if d[1] > 0])
        P.emit()
    return nc


def _fm_tiles(W):
    K, N = W.shape
    return np.ascontiguousarray(W.reshape(K // 128, 128, N // 128, 128).transpose(2, 1, 0, 3).reshape(N // 128, 128, (K // 128) * 128))


def _rope_tables():
    f32 = np.float32
    pos = np.arange(TPOS, dtype=f32)

    def tab(rot_dim, blk):
        half = rot_dim // 2
        inv = (f32(500000.0) ** (-(np.arange(0, rot_dim, 2, dtype=f32)) / f32(rot_dim))).astype(f32)
        ang = (pos[:, None] * inv[None, :]).astype(f32)
        cos = np.cos(ang).astype(f32).T
        sin = np.sin(ang).astype(f32).T
        ct = np.ones((128, TPOS), f32)
        st = np.zeros((128, TPOS), f32)
        for b in range(0, 128, blk):
            ct[b:b + half] = cos
            ct[b + half:b + 2 * half] = cos
            st[b:b + half] = -sin
            st[b + half:b + 2 * half] = sin
        return ct, st

    cA, sA = tab(32, 128)
    cI, sI = tab(16, 64)
    return np.stack([cA, sA, cI, sI]).astype(f32)


def _perms():
    f32 = np.float32
    pa = np.zeros((128, 128), f32)
    for m in range(16):
        pa[m + 16, m] = 1.0
        pa[m, m + 16] = 1.0
    pi = np.zeros((128, 128), f32)
    for b in (0, 64):
        for m in range(8):
            pi[b + m + 8, b + m] = 1.0
            pi[b + m, b + m + 8] = 1.0
    return np.stack([pa, pi, np.eye(128, dtype=f32)])


def prep_shared(inp):
    f32 = np.float32
    g = lambda k: np.asarray(inp[k], f32)[0]
    sh = {}
    for i, pre in ((1, "ffn1"), (2, "ffn2")):
        sh[f"wg{i}"] = _fm_tiles(g(f"{pre}_w_gate"))
        sh[f"wu{i}"] = _fm_tiles(g(f"{pre}_w_up"))
        sh[f"wd{i}"] = _fm_tiles(g(f"{pre}_w_down"))
    win = g("w_in")
    sp = np.cumsum([0, 1024, 256, 256, 1024, 64, 16, 1024, 1024, 1024, 2048, 2048])
    seg = lambda i: win[:, sp[i]:sp[i + 1]]
    q, k, v, qi, ki, wi, xc, gb, gc, ga, gc2 = [seg(i) for i in range(11)]
    fm = np.concatenate([k, ki, ki, q, qi, xc, gb, gc, ga, gc2], axis=1)
    assert fm.shape[1] == N_WIN * 128
    sh["win"] = _fm_tiles(fm)
    vw = np.concatenate([v, wi], axis=1)
    vw = vw.reshape(16, 128, 272).transpose(1, 0, 2)
    vwp = np.zeros((128, 18, 272), f32)
    vwp[:, :16] = vw
    sh["wvw"] = np.ascontiguousarray(vwp.reshape(128, 3, 6 * 272).transpose(1, 0, 2))
    sh["wat"] = _fm_tiles(g("w_attn_branch"))
    sh["wcv"] = _fm_tiles(g("w_conv_branch"))
    sh["wout"] = _fm_tiles(g("w_out"))
    cst = np.zeros((128, NCST), f32)
    cst[:, CG1:CG1 + 16] = g("ffn1_norm_g").reshape(16, 128).T
    cst[:, CGM:CGM + 16] = g("mix_norm_g").reshape(16, 128).T
    cst[:, CG2:CG2 + 16] = g("ffn2_norm_g").reshape(16, 128).T
    cst[:, CQG] = g("q_norm_g")
    cst[:, CKG] = g("k_norm_g")
    cw = g("conv_w")
    cst[:, CCW:CCW + 24] = cw.reshape(3, 8, 128).transpose(2, 1, 0).reshape(128, 24)
    cst[:, CCB:CCB + 8] = g("conv_b").reshape(8, 128).T
    cst[:, CPW:CPW + 32] = (2.0 ** -(np.arange(32, dtype=np.float64) + 1.0)).astype(f32)[None, :]
    sh["cst"] = cst
    sh["tab"] = _rope_tables()
    sh["prm"] = _perms()
    return sh


def prep_x(inp, b):
    f32 = np.float32
    h0 = np.concatenate([np.asarray(inp["meta_tokens"], f32), np.asarray(inp["x"][b], f32)], axis=0)
    return np.ascontiguousarray(h0.reshape(TPOS, 16, 128).transpose(2, 1, 0))


def kernel(**inputs):
    sh = prep_shared(inputs)
    nb = inputs["x"].shape[0]
    nc = build()
    in_maps = []
    for b in range(nb):
        m = dict(sh)
        m["xT"] = prep_x(inputs, b)
        in_maps.append(m)
    res = run_bass_kernel_spmd(nc, in_maps, core_ids=list(range(nb)))
    out = np.empty((nb, 2048, D), np.float32)
    for b in range(nb):
        o = res.results[b]["outT"]
        out[b] = o.transpose(2, 1, 0).reshape(2048, D)
    return out
```

```python
import math
from contextlib import ExitStack

import numpy as np
import concourse.bass as bass
import concourse.mybir as mybir
from concourse.bass_utils import run_bass_kernel_spmd

F32 = mybir.dt.float32
BF16 = mybir.dt.bfloat16
AF = mybir.ActivationFunctionType
ALU = mybir.AluOpType

D = 2048
DFF = 5632
NFF = DFF // 128
TPOS = 2064
NT = 688
NH = 344
NTILE = 3
SUBS = [(0, 128), (128, 128), (256, 88), (344, 128), (472, 128), (600, 88)]
NSUB = 18
NSLOT = 6
EPS = 1e-6
NEG = -30000.0
ENGS = ["pe", "act", "dve", "pool", "sp"]

C_K, C_KI, C_Q, C_QI, C_XC, C_GB, C_GC, C_GA, C_GC2 = 0, 2, 3, 11, 19, 27, 35, 43, 59
N_WIN = 75
CG1, CGM, CG2, CQG, CKG, CCW, CCB = 0, 16, 32, 48, 49, 50, 74
CPW = 82
NCST = 114
NBIS = 26


class Res:
    __slots__ = ("name", "writer", "readers", "ov")

    def __init__(self, name):
        self.name = name
        self.writer = None
        self.readers = {}
        self.ov = [self]


def alias(*rs):
    for a in rs:
        for b in rs:
            if b not in a.ov:
                a.ov.append(b)


class Prog:
    def __init__(self, nc, ctx):
        self.nc = nc
        self.ctx = ctx
        self.ops = {e: [] for e in ENGS}
        self.sems = {}
        self.cnt = {}
        self.waited = {e: {} for e in ENGS}
        for e in ENGS:
            self._sem("E_" + e)

    def _sem(self, key):
        if key not in self.sems:
            self.sems[key] = self.ctx.enter_context(self.nc.semaphore(key))
            self.cnt[key] = 0
        return self.sems[key]

    def _collect(self, eng, reads, writes, extra):
        deps = {}

        def add(d):
            if d is None:
                return
            k, v = d
            if deps.get(k, 0) < v:
                deps[k] = v

        for r in reads:
            for y in r.ov:
                add(y.writer)
        for w in writes:
            for y in w.ov:
                add(y.writer)
                for k, v in y.readers.items():
                    add((k, v))
        for d in extra:
            add(d)
        waits = []
        wd = self.waited[eng]
        for k, v in deps.items():
            if wd.get(k, 0) < v:
                wd[k] = v
                waits.append((self.sems[k], v))
        return waits

    def _commit(self, tick, reads, writes):
        for r in reads:
            if r not in writes:
                k, v = tick
                if r.readers.get(k, 0) < v:
                    r.readers[k] = v
        for w in writes:
            w.writer = tick
            w.readers = {}

    def op(self, eng, fn, reads=(), writes=(), extra=()):
        waits = self._collect(eng, reads, writes, extra)
        key = "E_" + eng
        self.cnt[key] += 1
        tick = (key, self.cnt[key])
        self.ops[eng].append((fn, waits, (self.sems[key], 1)))
        self._commit(tick, reads, writes)
        return tick

    def dma(self, queue, out, in_, sem, reads=(), writes=(), extra=(), **kw):
        self._sem(sem)
        waits = self._collect(queue, reads, writes, extra)
        self.cnt[sem] += 16
        tick = (sem, self.cnt[sem])

        def fn(e, out=out, in_=in_, kw=kw):
            return e.dma_start(out=out, in_=in_, **kw)

        self.ops[queue].append((fn, waits, (self.sems[sem], 16)))
        self._commit(tick, reads, writes)
        return tick

    def wait(self, eng, deps):
        waits = self._collect(eng, (), (), deps)
        self.ops[eng].append((None, waits, None))

    def emit(self):
        nc = self.nc
        with nc.Block() as block:
            def replay(name):
                def run(e):
                    for fn, waits, inc in self.ops[name]:
                        for s, v in waits:
                            e.wait_ge(s, v)
                        if fn is None:
                            continue
                        ins = fn(e)
                        if inc is not None:
                            ins.then_inc(inc[0], inc[1])
                return run
            block.tensor(replay("pe"))
            block.scalar(replay("act"))
            block.vector(replay("dve"))
            block.gpsimd(replay("pool"))
            block.sync(replay("sp"))


def build(n_tiles=NTILE, dbg=None):
    nc = bass.Bass("TRN2", target_bir_lowering=False)
    dram = lambda n, s: nc.dram_tensor(n, s, F32, kind="ExternalInput").ap()
    xT = dram("xT", [128, 16, TPOS])
    WG = [dram("wg1", [NFF, 128, 2048]), dram("wg2", [NFF, 128, 2048])]
    WU = [dram("wu1", [NFF, 128, 2048]), dram("wu2", [NFF, 128, 2048])]
    WD = [dram("wd1", [16, 128, DFF]), dram("wd2", [16, 128, DFF])]
    WIN = dram("win", [N_WIN, 128, 2048])
    WVW = dram("wvw", [3, 128, 6 * 272])
    WAT = dram("wat", [16, 128, 1024])
    WCV = dram("wcv", [16, 128, 1024])
    WOUT = dram("wout", [16, 128, 2048])
    CST = dram("cst", [128, NCST])
    TAB = dram("tab", [4, 128, TPOS])
    PRM = dram("prm", [3, 128, 128])
    outT = nc.dram_tensor("outT", [128, 16, 2048], F32, kind="ExternalOutput").ap()
    dbg_out = None
    if dbg:
        dbg_out = nc.dram_tensor("dbg", [128, dbg], F32, kind="ExternalOutput").ap()

    with ExitStack() as ctx:
        P = Prog(nc, ctx)
        layout = {}
        cur = [0]

        def carve(name, nbytes, at=None):
            nb = (nbytes + 63) // 64 * 64
            lo = cur[0] if at is None else at
            layout[name] = (lo, lo + nb)
            if at is None:
                cur[0] += nb
            return lo

        carve("h", 16 * NT * 4)
        carve("u", 16 * NT * 2)
        for i in range(NSLOT):
            carve(f"ws{i}", 4096)
        carve("kT", 2 * TPOS * 2)
        carve("vtok", NSUB * 256 * 2)
        carve("kiT", TPOS * 2)
        carve("cst", NCST * 4)
        carve("perm", 2 * 128 * 4)
        carve("ident", 128 * 2)
        carve("ones", 128 * 2)
        carve("negc", 128 * 2)
        carve("epsc", 64)
        carve("halo", 8 * 2 * 4)
        carve("wtok", 6 * 16 * 4)
        for i in range(2):
            carve(f"bsc{i}", 64 * 4)
            carve(f"pw{i}", 32 * 4)
        for i in range(7):
            carve(f"T{i}", 696 * 4)
        for i in range(2):
            carve(f"sq{i}", NT * 2)
        for i in range(2):
            carve(f"rl{i}", 512 * 4)
        carve("mb3", TPOS * 2)
        for i in range(4):
            carve(f"pt{i}", NH * 2)
        carve("ycx0", NT * 2); carve("ycx1", NT * 2)
        a_lo = carve("a", NFF * NT * 2)
        off = a_lo
        for nm, nb in (("qT", 8 * NT * 2), ("qiT", 8 * NT * 2), ("yab", 8 * NT * 2),
                       ("isc0", TPOS * 4), ("isc1", TPOS * 4), ("mb0", TPOS * 2), ("mb1", TPOS * 2)):
            carve(nm, nb, at=off)
            off = layout[nm][1]
        assert off <= layout["a"][1], (off, layout["a"])
        layout["merged"] = (layout["qT"][0], layout["qiT"][1])
        layout["mb2"] = (layout["T1"][0], layout["T1"][0] + 4160)
        assert layout["mb2"][1] <= layout["T2"][1]
        layout["junk"] = (layout["T3"][0], layout["T3"][0] + 4160)
        assert layout["junk"][1] <= layout["T4"][1]
        for i in range(3):
            layout[f"Tb{i}"] = (layout["isc0"][0] + i * NT * 4, layout["isc0"][0] + (i + 1) * NT * 4)
        assert layout["Tb2"][1] <= layout["isc0"][1]
        for pre in ("T", "Tb"):
            for i in range(3):
                for hf in range(2):
                    lo = layout[f"{pre}{i}"][0] + hf * NH * 4
                    layout[f"{pre}{i}h{hf}"] = (lo, lo + NH * 4)
        for c in range(8):
            for hf in range(2):
                for nm in ("qT", "qiT"):
                    lo = layout[nm][0] + c * NT * 2 + hf * NH * 2
                    layout[f"{nm}{c}h{hf}"] = (lo, lo + NH * 2)
        for m in range(16):
            for hf in range(2):
                lo = layout["merged"][0] + m * NT * 2 + hf * NH * 2
                layout[f"mg{m}h{hf}"] = (lo, lo + NH * 2)
        for c in range(16):
            layout[f"h{c}"] = (layout["h"][0] + c * NT * 4, layout["h"][0] + (c + 1) * NT * 4)
            layout[f"u{c}"] = (layout["u"][0] + c * NT * 2, layout["u"][0] + (c + 1) * NT * 2)
        for p_ in range(2):
            for hf in range(2):
                lo = layout[f"sq{p_}"][0] + hf * NH * 2
                layout[f"sq{p_}h{hf}"] = (lo, lo + NH * 2)
        homes = [layout["T5"][0], layout["T5"][0] + NT * 2, layout["T6"][0], layout["T6"][0] + NT * 2,
                 layout["sq0"][0], layout["sq1"][0], layout["ycx0"][0], layout["ycx1"][0]]
        for c, lo in enumerate(homes):
            layout[f"ycb{c}"] = (lo, lo + NT * 2)
        total = cur[0]
        assert total <= 212800, total
        arena = ctx.enter_context(nc.sbuf_tensor("arena", [128, total // 2], BF16))

        def vb(name):
            lo, hi = layout[name]
            return arena[:, lo // 2: hi // 2]

        def vf(name):
            lo, hi = layout[name]
            return arena[:, lo // 2: hi // 2].bitcast(F32)

        R = {n: Res(n) for n in layout}
        names = list(layout)
        for i, a_ in enumerate(names):
            for b_ in names[i + 1:]:
                la, lb = layout[a_], layout[b_]
                if la[0] < lb[1] and lb[0] < la[1]:
                    alias(R[a_], R[b_])

        h = vf("h"); u = vb("u"); a = vb("a")
        ws = [vb(f"ws{i}") for i in range(NSLOT)]
        kT = vb("kT"); vtok = vb("vtok"); kiT = vb("kiT")
        cst = vf("cst"); perm = vf("perm"); ident = vb("ident"); ones = vb("ones"); negc = vb("negc")
        epsc = vf("epsc"); halo = vf("halo"); wtok = vf("wtok")
        bsc = [vf("bsc0"), vf("bsc1")]; pw = [vf("pw0"), vf("pw1")]
        Rbsc = [R["bsc0"], R["bsc1"]]; Rpw = [R["pw0"], R["pw1"]]
        T = [vf(f"T{i}") for i in range(7)]
        sq = [vb(f"sq{i}") for i in range(2)]
        rl = [vf(f"rl{i}") for i in range(2)]
        pt = [vb(f"pt{i}") for i in range(4)]
        qT = vb("qT"); qiT = vb("qiT"); yab = vb("yab")
        iscs = [vf("isc0"), vf("isc1")]; Risc = [R["isc0"], R["isc1"]]
        junk = vb("junk")
        mb = [vb(f"mb{i}") for i in range(4)]
        MBI = [0, 1, 2, 3, 0, 1]
        merged = vb("merged")
        ycbv = [vb(f"ycb{c}") for c in range(8)]
        Rycb = [R[f"ycb{c}"] for c in range(8)]
        TT = [[T[0], T[1], T[2]], [vf("Tb0"), vf("Tb1"), vf("Tb2")]]
        TPRE = ["T", "Tb"]
        RT = [R[f"T{i}"] for i in range(7)]
        Rsq = [R[f"sq{i}"] for i in range(2)]
        Rrl = [R[f"rl{i}"] for i in range(2)]
        Rpt = [R[f"pt{i}"] for i in range(4)]
        Rws = [R[f"ws{i}"] for i in range(NSLOT)]
        Rmb = [R[f"mb{i}"] for i in range(4)]

        Rh = [R[f"h{c}"] for c in range(16)]
        Ru = [R[f"u{c}"] for c in range(16)]
        ps = [ctx.enter_context(nc.psum_tensor(f"ps{i}", [128, 512], F32)) for i in range(8)]
        Rps = [Res(f"ps{i}") for i in range(8)]

        dbg_col = [0]

        def dump(ap, res, ncols):
            if dbg_out is None:
                return
            c0 = dbg_col[0]
            np_ = ap.shape[0]
            P.dma("pool", dbg_out[0:np_, c0:c0 + ncols], ap, "st_dbg", reads=[res])
            dbg_col[0] += ncols
            return c0

        wq = []
        wstate = {"issued": 0, "used": 0, "rel": 0}

        def wplan(ap, ncols):
            wq.append((ap, ncols))

        def wissue_upto(k):
            while wstate["issued"] < min(k, len(wq)):
                i = wstate["issued"]
                ap, ncols = wq[i]
                s = i % NSLOT
                P.dma("pool", ws[s][:, 0:ncols], ap, f"ld_w{s}", writes=[Rws[s]], max_dma_last_dim=4096)
                wstate["issued"] += 1

        def wnext():
            i = wstate["used"]
            wissue_upto(wstate["rel"] + NSLOT)
            assert wstate["issued"] > i, "weight ring over-subscribed"
            wstate["used"] += 1
            return i % NSLOT

        def wrel(n):
            wstate["rel"] += n
            assert wstate["rel"] <= wstate["used"]
            wissue_upto(wstate["rel"] + NSLOT)

        P.dma("sp", cst[:, 0:NCST], CST[:, :], "ld_c0", writes=[R["cst"]])
        P.dma("sp", perm[:, 0:128], PRM[0], "ld_c1", writes=[R["perm"]])
        P.dma("sp", perm[:, 128:256], PRM[1], "ld_c2", writes=[R["perm"]])
        P.dma("pool", ident[:, :], PRM[2], "ld_id", writes=[R["ident"]])
        P.op("pool", lambda e: e.memset(ones[:, :], 1.0), writes=[R["ones"]])
        P.op("pool", lambda e: e.memset(negc[:, :], NEG), writes=[R["negc"]])
        P.op("pool", lambda e: e.memset(epsc[:, :], EPS), writes=[R["epsc"]])
        P.op("pool", lambda e: e.memset(halo[:, :], 0.0), writes=[R["halo"]])

        def mm_group(bank, M, N, pairs, reads, col0=0):
            def fn(e, bank=bank, M=M, N=N, pairs=pairs, col0=col0):
                ins = None
                n = len(pairs)
                for i, (l, r) in enumerate(pairs):
                    ins = e.matmul(ps[bank][0:M, col0:col0 + N], lhsT=l, rhs=r, start=(i == 0), stop=(i == n - 1))
                return ins
            return P.op("pe", fn, reads=reads, writes=[Rps[bank]])

        def rmsnorm(gcol):
            for c in range(16):
                P.op("act", lambda e, c=c: e.activation(out=sq[c % 2][:, 0:NT], in_=h[:, c * NT:(c + 1) * NT], func=AF.Square),
                     reads=[Rh[c]], writes=[Rsq[c % 2]])
                for hf in range(2):
                    def fn(e, c=c, hf=hf):
                        return e.matmul(ps[hf][:, 0:NH], lhsT=ones[:, :], rhs=sq[c % 2][:, hf * NH:(hf + 1) * NH],
                                        start=(c == 0), stop=(c == 15))
                    P.op("pe", fn, reads=[Rsq[c % 2], R["ones"]], writes=[Rps[hf]])
            for hf in range(2):
                P.op("act", lambda e, hf=hf: e.activation(out=T[2][:, hf * NH:(hf + 1) * NH], in_=ps[hf][:, 0:NH], func=AF.Ln,
                                                          bias=epsc[:, 0:1], scale=1.0 / D),
                     reads=[Rps[hf], R["epsc"]], writes=[RT[2]])
            P.op("act", lambda e: e.activation(out=T[2][:, 0:NT], in_=T[2][:, 0:NT], func=AF.Exp, scale=-0.5), reads=[RT[2]], writes=[RT[2]])
            for c in range(16):
                P.op("dve", lambda e, c=c: e.scalar_tensor_tensor(out=u[:, c * NT:(c + 1) * NT], in0=h[:, c * NT:(c + 1) * NT],
                                                                  scalar=cst[:, gcol + c:gcol + c + 1], in1=T[2][:, 0:NT],
                                                                  op0=ALU.mult, op1=ALU.mult),
                     reads=[Rh[c], RT[2], R["cst"]], writes=[Ru[c]])

        def ffn(which, gcol, store_tile=None):
            rmsnorm(gcol)
            for j in range(NFF):
                sg_ = wnext(); su_ = wnext()
                b0 = (j % 2) * 4
                for (s_, bb) in ((sg_, b0), (su_, b0 + 2)):
                    for hf in range(2):
                        pairs = [(ws[s_][:, k * 128:(k + 1) * 128], u[:, k * NT + hf * NH: k * NT + (hf + 1) * NH]) for k in range(16)]
                        if j == 0 and s_ == sg_ and hf == 0:
                            for k in range(16):
                                P.op("pe", lambda e, k=k, bank=bb + hf, l=pairs[k][0], r=pairs[k][1]: e.matmul(
                                    ps[bank][:, 0:NH], lhsT=l, rhs=r, start=(k == 0), stop=(k == 15)),
                                    reads=[Rws[s_], Ru[k]], writes=[Rps[bb + hf]])
                        else:
                            mm_group(bb + hf, 128, NH, pairs, [Rws[s_], R["u"]])
                wrel(2)
                q_ = j % 2
                for hf in range(2):
                    P.op("act", lambda e, q_=q_, hf=hf, b=b0 + hf: e.activation(out=T[q_][:, hf * NH:(hf + 1) * NH], in_=ps[b][:, 0:NH], func=AF.Silu),
                         reads=[Rps[b0 + hf]], writes=[RT[q_]])
                    P.op("dve", lambda e, q_=q_, hf=hf, b=b0 + 2 + hf, j=j: e.tensor_tensor(
                        out=a[:, j * NT + hf * NH: j * NT + (hf + 1) * NH], in0=ps[b][:, 0:NH], in1=T[q_][:, hf * NH:(hf + 1) * NH], op=ALU.mult),
                        reads=[Rps[b0 + 2 + hf], RT[q_]], writes=[R["a"]])
            for m in range(16):
                sl = [wnext() for _ in range(3)]
                for hf in range(2):
                    bank = (m % 4) * 2 + hf
                    pairs = [(ws[sl[hc // 16]][:, (hc % 16) * 128:(hc % 16 + 1) * 128], a[:, hc * NT + hf * NH: hc * NT + (hf + 1) * NH])
                             for hc in range(NFF)]
                    mm_group(bank, 128, NH, pairs, [Rws[s] for s in sl] + [R["a"]])
                    if hf == 1:
                        wrel(3)
                    P.op("dve", lambda e, m=m, hf=hf, bank=bank: e.scalar_tensor_tensor(
                        out=h[:, m * NT + hf * NH: m * NT + (hf + 1) * NH], in0=ps[bank][:, 0:NH], scalar=0.5,
                        in1=h[:, m * NT + hf * NH: m * NT + (hf + 1) * NH], op0=ALU.mult, op1=ALU.add),
                        reads=[Rps[bank], Rh[m]], writes=[Rh[m]])
                if store_tile is not None:
                    p0_ = store_tile * NT
                    if store_tile == 0:
                        P.dma("sp", outT[:, m, 0:NT - 16], h[:, m * NT + 16:(m + 1) * NT], f"st_o{m}", reads=[Rh[m]])
                    else:
                        P.dma("sp", outT[:, m, p0_ - 16:p0_ - 16 + NT], h[:, m * NT:(m + 1) * NT], f"st_o{m}", reads=[Rh[m]])

        def plan_ffn(which):
            for j in range(NFF):
                wplan(WG[which][j], 2048); wplan(WU[which][j], 2048)
            for m in range(16):
                wplan(WD[which][m][:, 0:2048], 2048); wplan(WD[which][m][:, 2048:4096], 2048); wplan(WD[which][m][:, 4096:DFF], 1536)

        def plan_mixer():
            for i in range(3):
                wplan(WVW[i], 6 * 272)
            for c in (C_K, C_K + 1, C_KI):
                wplan(WIN[c], 2048)
            for c in range(8):
                wplan(WIN[C_Q + c], 2048)
            for c in range(8):
                wplan(WIN[C_QI + c], 2048)
            for c in range(8):
                wplan(WIN[C_XC + c], 2048); wplan(WIN[C_GC + c], 2048); wplan(WIN[C_GB + c], 2048)
            for hf in range(2):
                for m in range(16):
                    wplan(WIN[C_GA + m], 2048); wplan(WIN[C_GC2 + m], 2048); wplan(WAT[m], 1024); wplan(WCV[m], 1024)
                for m in range(16):
                    wplan(WOUT[m], 2048)

        for it in range(n_tiles):
            plan_ffn(0); plan_mixer(); plan_ffn(1)

        def proj_chunk(slot, bank0):
            for hf in range(2):
                pairs = [(ws[slot][:, k * 128:(k + 1) * 128], u[:, k * NT + hf * NH: k * NT + (hf + 1) * NH]) for k in range(16)]
                mm_group(bank0 + hf, 128, NH, pairs, [Rws[slot], R["u"]])
            wrel(1)

        SHUF = [list(range(16, 32)) + list(range(0, 16)), list(range(8, 16)) + list(range(0, 8)) + list(range(16, 32))]

        def rope_chunk(bank0, do_norm, gcol, tabi, permi, dst, dst_res, dst_col, par):
            A, B, C = TT[par]
            for hf in range(2):
                RA, RB, RC = [R[f"{TPRE[par]}{i}h{hf}"] for i in range(3)]
                Rq = R[f"sq{par}h{hf}"]
                sqv = sq[par]
                b = bank0 + hf
                b2 = bank0 + 2 + hf
                hs = slice(hf * NH, (hf + 1) * NH)
                if do_norm:
                    P.op("act", lambda e, hs=hs, b=b: e.activation(out=sqv[:, hs], in_=ps[b][:, 0:NH], func=AF.Square),
                         reads=[Rps[b]], writes=[Rq])
                    P.op("pe", lambda e, hs=hs, b2=b2: e.matmul(ps[b2][:, 0:NH], lhsT=ones[:, :], rhs=sqv[:, hs], start=True, stop=True),
                         reads=[Rq, R["ones"]], writes=[Rps[b2]])
                    P.op("act", lambda e, hs=hs, b2=b2: e.activation(out=A[:, hs], in_=ps[b2][:, 0:NH], func=AF.Ln, bias=epsc[:, 0:1], scale=1.0 / 128),
                         reads=[Rps[b2], R["epsc"]], writes=[RA])
                    P.op("act", lambda e, hs=hs: e.activation(out=A[:, hs], in_=A[:, hs], func=AF.Exp, scale=-0.5), reads=[RA], writes=[RA])
                    P.op("dve", lambda e, hs=hs, b=b: e.scalar_tensor_tensor(out=B[:, hs], in0=ps[b][:, 0:NH], scalar=cst[:, gcol:gcol + 1],
                                                                            in1=A[:, hs], op0=ALU.mult, op1=ALU.mult),
                         reads=[Rps[b], RA, R["cst"]], writes=[RB])
                    P.op("dve", lambda e, hs=hs: e.stream_shuffle(out=C[:, hs], in_=B[:, hs], mask=SHUF[permi]), reads=[RB], writes=[RC])
                    P.op("dve", lambda e, hs=hs: e.tensor_tensor(out=B[:, hs], in0=B[:, hs], in1=T[3 + 2 * tabi][:, hs], op=ALU.mult),
                         reads=[RB, RT[3 + 2 * tabi]], writes=[RB])
                else:
                    P.op("act", lambda e, hs=hs, b=b: e.activation(out=C[:, hs], in_=ps[b][:, 0:NH], func=AF.Copy), reads=[Rps[b]], writes=[RC])
                    P.op("dve", lambda e, hs=hs: e.stream_shuffle(out=C[:, hs], in_=C[:, hs], mask=SHUF[permi]), reads=[RC], writes=[RC])
                    P.op("dve", lambda e, hs=hs, b=b: e.tensor_tensor(out=B[:, hs], in0=ps[b][:, 0:NH], in1=T[3 + 2 * tabi][:, hs], op=ALU.mult),
                         reads=[Rps[b], RT[3 + 2 * tabi]], writes=[RB])
                P.op("dve", lambda e, hs=hs: e.tensor_tensor(out=C[:, hs], in0=C[:, hs], in1=T[4 + 2 * tabi][:, hs], op=ALU.mult),
                     reads=[RC, RT[4 + 2 * tabi]], writes=[RC])
                P.op("dve", lambda e, hs=hs, hf=hf: e.tensor_tensor(out=dst[:, dst_col + hf * NH: dst_col + (hf + 1) * NH], in0=C[:, hs], in1=B[:, hs], op=ALU.add),
                     reads=[RB, RC], writes=[dst_res[hf] if isinstance(dst_res, list) else dst_res])

        rot = {"sbank": 0, "grp": 0}

        def idx_scores(it, j, q, banks=(4, 5, 6, 7)):
            p0 = it * NT
            o, S = SUBS[j]
            L = p0 + o + S
            isc = iscs[q]
            nblk = (L + 511) // 512
            for blk in range(nblk):
                c0 = blk * 512
                W = min(512, L - c0)
                for hd in range(16):
                    bank = banks[rot["sbank"] % len(banks)]
                    r_ = rot["sbank"] % 2
                    rot["sbank"] += 1
                    pr = slice((hd % 2) * 64, (hd % 2) * 64 + 64)
                    qc = (hd // 2) * NT + o
                    P.op("pe", lambda e, bank=bank, S=S, W=W, pr=pr, qc=qc, c0=c0: e.matmul(
                        ps[bank][0:S, 0:W], lhsT=qiT[pr, qc:qc + S], rhs=kiT[pr, c0:c0 + W], start=True, stop=True),
                        reads=[R[f"qiT{hd // 2}h{j // 3}"], R["kiT"]], writes=[Rps[bank]])
                    P.op("act", lambda e, bank=bank, r_=r_, S=S, W=W: e.activation(out=rl[r_][0:S, 0:W], in_=ps[bank][0:S, 0:W], func=AF.Relu),
                         reads=[Rps[bank]], writes=[Rrl[r_]])
                    wc = j * 16 + hd
                    if hd == 0:
                        P.op("dve", lambda e, r_=r_, S=S, W=W, c0=c0, wc=wc: e.tensor_scalar(
                            out=isc[0:S, c0:c0 + W], in0=rl[r_][0:S, 0:W], scalar1=wtok[0:S, wc:wc + 1], scalar2=None, op0=ALU.mult),
                            reads=[Rrl[r_], R["wtok"]], writes=[Risc[q]])
                    else:
                        P.op("dve", lambda e, r_=r_, S=S, W=W, c0=c0, wc=wc: e.scalar_tensor_tensor(
                            out=isc[0:S, c0:c0 + W], in0=rl[r_][0:S, 0:W], scalar=wtok[0:S, wc:wc + 1],
                            in1=isc[0:S, c0:c0 + W], op0=ALU.mult, op1=ALU.add),
                            reads=[Rrl[r_], R["wtok"], Risc[q]], writes=[Risc[q]])
                yield

        def bis_setup(it, j, q):
            p0 = it * NT
            o, S = SUBS[j]
            L = p0 + o + S
            isc = iscs[q]; b = bsc[q]; Rb = Rbsc[q]
            P.op("dve", lambda e: e.max(out=b[0:S, 16:24], in_=isc[0:S, 0:L]), reads=[Risc[q]], writes=[Rb])
            P.op("act", lambda e: e.activation(out=junk[0:S, 0:L], in_=isc[0:S, 0:L], func=AF.Copy, scale=-1.0), reads=[Risc[q]], writes=[R["junk"]])
            P.op("dve", lambda e: e.max(out=b[0:S, 8:16], in_=junk[0:S, 0:L]), reads=[R["junk"]], writes=[Rb])
            P.op("pool", lambda e: e.affine_select(out=isc[0:S, L - S:L], in_=isc[0:S, L - S:L], pattern=[[-1, S]],
                                                   compare_op=ALU.is_ge, fill=-1.0e30, base=0, channel_multiplier=1),
                 reads=[Risc[q]], writes=[Risc[q]])
            P.op("dve", lambda e: e.tensor_scalar(out=b[0:S, 5:6], in0=b[0:S, 8:9], scalar1=1.0 + 2.0 ** -7, scalar2=None, op0=ALU.mult), reads=[Rb], writes=[Rb])
            P.op("dve", lambda e: e.scalar_tensor_tensor(out=b[0:S, 4:5], in0=b[0:S, 8:9], scalar=1.0 - 2.0 ** -7, in1=b[0:S, 5:6], op0=ALU.mult, op1=ALU.max),
                 reads=[Rb], writes=[Rb])
            P.op("dve", lambda e: e.tensor_tensor(out=b[0:S, 6:7], in0=b[0:S, 16:17], in1=b[0:S, 4:5], op=ALU.add), reads=[Rb], writes=[Rb])
            P.op("dve", lambda e: e.tensor_tensor(out=b[0:S, 7:8], in0=b[0:S, 16:17], in1=b[0:S, 4:5], op=ALU.subtract), reads=[Rb], writes=[Rb])
            P.op("dve", lambda e: e.tensor_scalar(out=b[0:S, 1:2], in0=b[0:S, 7:8], scalar1=0.5, scalar2=None, op0=ALU.mult), reads=[Rb], writes=[Rb])
            P.op("dve", lambda e: e.tensor_scalar(out=pw[q][0:S, 0:32], in0=cst[0:S, CPW:CPW + 32], scalar1=b[0:S, 6:7], scalar2=None, op0=ALU.mult),
                 reads=[Rb, R["cst"]], writes=[Rpw[q]])

        Rjunkd = Res("junkd")

        def bis_iter(it, j, q, i):
            p0 = it * NT
            o, S = SUBS[j]
            L = p0 + o + S
            isc = iscs[q]; b = bsc[q]; Rb = Rbsc[q]
            if q == 1:
                P.op("act", lambda e: e.activation(out=junk[0:S, 0:L], in_=isc[0:S, 0:L], func=AF.Sign, bias=b[0:S, 1:2], scale=-1.0, accum_out=b[0:S, 0:1]),
                     reads=[Risc[q], Rb], writes=[R["junk"], Rb])
                P.op("dve", lambda e: e.tensor_scalar(out=b[0:S, 2:3], in0=b[0:S, 0:1], scalar1=float(L - 511), scalar2=0.5, op0=ALU.is_le, op1=ALU.subtract),
                     reads=[Rb], writes=[Rb])
            else:
                P.op("dve", lambda e: e.tensor_scalar(out=junk[0:S, 0:L], in0=isc[0:S, 0:L], scalar1=b[0:S, 1:2], scalar2=0.0, op0=ALU.is_ge, op1=ALU.add,
                                                      accum_out=b[0:S, 0:1]),
                     reads=[Risc[q], Rb], writes=[Rjunkd, Rb])
                P.op("dve", lambda e: e.tensor_scalar(out=b[0:S, 2:3], in0=b[0:S, 0:1], scalar1=256.0, scalar2=0.5, op0=ALU.is_ge, op1=ALU.subtract),
                     reads=[Rb], writes=[Rb])
            P.op("dve", lambda e: e.scalar_tensor_tensor(out=b[0:S, 1:2], in0=b[0:S, 2:3], scalar=pw[q][0:S, i:i + 1], in1=b[0:S, 1:2], op0=ALU.mult, op1=ALU.add),
                 reads=[Rb, Rpw[q]], writes=[Rb])

        def bis_final(it, j, q):
            p0 = it * NT
            o, S = SUBS[j]
            L = p0 + o + S
            jj = MBI[j]
            if L > 256:
                isc = iscs[q]; b = bsc[q]; Rb = Rbsc[q]
                P.op("dve", lambda e: e.scalar_tensor_tensor(out=b[0:S, 3:4], in0=pw[q][0:S, NBIS:NBIS + 1], scalar=-1.0, in1=b[0:S, 1:2], op0=ALU.mult, op1=ALU.add),
                     reads=[Rb, Rpw[q]], writes=[Rb])
                P.op("dve", lambda e: e.tensor_scalar(out=mb[jj][0:S, 0:L], in0=isc[0:S, 0:L], scalar1=b[0:S, 3:4], scalar2=NEG, op0=ALU.is_lt, op1=ALU.mult),
                     reads=[Risc[q], Rb], writes=[Rmb[jj]])
            else:
                P.op("dve", lambda e: e.memset(mb[jj][0:S, 0:L], 0.0), writes=[Rmb[jj]])
            P.op("pool", lambda e: e.affine_select(out=mb[jj][0:S, L - S:L], in_=mb[jj][0:S, L - S:L], pattern=[[-1, S]],
                                                   compare_op=ALU.is_ge, fill=NEG, base=0, channel_multiplier=1),
                 reads=[Rmb[jj]], writes=[Rmb[jj]])

        def needs_bis(it, j):
            o, S = SUBS[j]
            return it * NT + o + S > 256

        def attn_group(it, hf, hd, sbanks=(4, 5, 6, 7), LA=2):
            nch = it * 6 + hf * 3 + 3
            gq = hd // 4
            ob = rot["grp"] % 2
            sb_ = 2 + rot["grp"] % 2
            rot["grp"] += 1
            q0 = hd * NT + hf * NH
            units = []
            for c in range(nch):
                ti, jc = divmod(c, 6)
                oc, Sc = SUBS[jc]
                units.append((c, ti * NT + oc, Sc))
            nu = len(units)

            base = it * 6 + hf * 3

            def col0(c):
                jj0 = max(0, c - base)
                return jj0, SUBS[hf * 3 + jj0][0] - hf * NH

            def emit_qk(c, pc, Sc, k):
                bank = sbanks[k % len(sbanks)]
                jj0, cs0 = col0(c)

                def fn(e):
                    e.matmul(ps[bank][0:Sc, cs0:NH], lhsT=kT[:, gq * TPOS + pc: gq * TPOS + pc + Sc], rhs=qT[:, q0 + cs0:q0 + NH], start=True, stop=False)
                    ins = None
                    for jj in range(jj0, 3):
                        o_, S_ = SUBS[hf * 3 + jj]
                        cs = o_ - hf * NH
                        l = mb[MBI[hf * 3 + jj]][0:S_, pc:pc + Sc]
                        ins = e.matmul(ps[bank][0:Sc, cs:cs + S_], lhsT=l, rhs=ident[0:S_, 0:S_], start=False, stop=(jj == 2))
                    return ins
                P.op("pe", fn, reads=[R["kT"], R[f"qT{hd}h{hf}"], R["ident"]] + Rmb, writes=[Rps[bank]])
                P.op("act", lambda e: e.activation(out=pt[k % 4][0:Sc, cs0:NH], in_=ps[bank][0:Sc, cs0:NH], func=AF.Exp, scale=128.0 ** -0.5),
                     reads=[Rps[bank]], writes=[Rpt[k % 4]])

            def emit_pv(c, pc, Sc, k):
                first = (k == 0)
                last = (k == nu - 1)
                jj0, cs0 = col0(c)
                assert not (first and cs0 != 0)

                def fn(e):
                    e.matmul(ps[ob][:, cs0:NH], lhsT=vtok[0:Sc, c * 256 + gq * 128: c * 256 + gq * 128 + 128], rhs=pt[k % 4][0:Sc, cs0:NH], start=first, stop=last)
                    return e.matmul(ps[sb_][:, cs0:NH], lhsT=ones[0:Sc, :], rhs=pt[k % 4][0:Sc, cs0:NH], start=first, stop=last)
                P.op("pe", fn, reads=[R["vtok"], Rpt[k % 4], R["ones"]], writes=[Rps[ob], Rps[sb_]])

            for k in range(nu + LA):
                if k < nu:
                    emit_qk(*units[k], k)
                if k >= LA:
                    emit_pv(*units[k - LA], k - LA)
            P.op("act", lambda e: e.activation(out=T[0][:, 0:NH], in_=ps[sb_][:, 0:NH], func=AF.Ln), reads=[Rps[sb_]], writes=[RT[0]])
            P.op("act", lambda e: e.activation(out=T[0][:, 0:NH], in_=T[0][:, 0:NH], func=AF.Exp, scale=-1.0), reads=[RT[0]], writes=[RT[0]])
            P.op("dve", lambda e: e.tensor_tensor(out=yab[:, q0:q0 + NH], in0=ps[ob][:, 0:NH], in1=T[0][:, 0:NH], op=ALU.mult),
                 reads=[Rps[ob], RT[0]], writes=[R["yab"]])

        def conv_chunk(c):
            sx = wnext(); sc_ = wnext(); sg_ = wnext()
            proj_chunk(sx, 0); proj_chunk(sc_, 2)
            for hf in range(2):
                P.op("act", lambda e, hf=hf: e.activation(out=T[0][:, hf * NH:(hf + 1) * NH], in_=ps[hf][:, 0:NH], func=AF.Copy),
                     reads=[Rps[hf]], writes=[RT[0]])
            proj_chunk(sg_, 0)
            P.op("dve", lambda e: e.tensor_copy(out=T[1][:, 0:2], in_=halo[:, c * 2:c * 2 + 2]), reads=[R["halo"]], writes=[RT[1]])
            for hf in range(2):
                P.op("dve", lambda e, hf=hf: e.tensor_tensor(out=T[1][:, 2 + hf * NH: 2 + (hf + 1) * NH], in0=ps[2 + hf][:, 0:NH],
                                                             in1=T[0][:, hf * NH:(hf + 1) * NH], op=ALU.mult),
                     reads=[Rps[2 + hf], RT[0]], writes=[RT[1]])
            P.op("act", lambda e: e.activation(out=halo[:, c * 2:c * 2 + 2], in_=T[1][:, NT:NT + 2], func=AF.Copy),
                 reads=[RT[1]], writes=[R["halo"]])
            P.op("act", lambda e: e.activation(out=T[2][:, 0:NT], in_=T[1][:, 2:2 + NT], func=AF.Identity,
                                               bias=cst[:, CCB + c:CCB + c + 1], scale=cst[:, CCW + c * 3 + 2:CCW + c * 3 + 3]),
                 reads=[RT[1], R["cst"]], writes=[RT[2]])
            for tap in (1, 0):
                P.op("dve", lambda e, tap=tap: e.scalar_tensor_tensor(out=T[2][:, 0:NT], in0=T[1][:, tap:tap + NT],
                                                                      scalar=cst[:, CCW + c * 3 + tap:CCW + c * 3 + tap + 1],
                                                                      in1=T[2][:, 0:NT], op0=ALU.mult, op1=ALU.add),
                     reads=[RT[1], RT[2], R["cst"]], writes=[RT[2]])
            for hf in range(2):
                P.op("dve", lambda e, hf=hf: e.tensor_tensor(out=ycbv[c][:, hf * NH:(hf + 1) * NH], in0=ps[hf][:, 0:NH],
                                                             in1=T[2][:, hf * NH:(hf + 1) * NH], op=ALU.mult),
                     reads=[Rps[hf], RT[2]], writes=[Rycb[c]])

        def merge_m(m, hf, b0):
            hs = slice(hf * NH, (hf + 1) * NH)
            for bb in (b0, b0 + 1):
                s_ = wnext()
                pairs = [(ws[s_][:, k * 128:(k + 1) * 128], u[:, k * NT + hf * NH: k * NT + (hf + 1) * NH]) for k in range(16)]
                mm_group(bb, 128, NH, pairs, [Rws[s_], R["u"]])
                wrel(1)
            sat = wnext()
            pairs = [(ws[sat][:, k * 128:(k + 1) * 128], yab[:, k * NT + hf * NH: k * NT + (hf + 1) * NH]) for k in range(8)]
            mm_group(b0 + 2, 128, NH, pairs, [Rws[sat], R["yab"]])
            wrel(1)
            scv = wnext()
            pairs = [(ws[scv][:, k * 128:(k + 1) * 128], ycbv[k][:, hf * NH:(hf + 1) * NH]) for k in range(8)]
            mm_group(b0 + 3, 128, NH, pairs, [Rws[scv]] + Rycb)
            wrel(1)
            P.op("act", lambda e: e.activation(out=T[0][:, hs], in_=ps[b0][:, 0:NH], func=AF.Sigmoid), reads=[Rps[b0]], writes=[RT[0]])
            P.op("act", lambda e: e.activation(out=T[1][:, hs], in_=ps[b0 + 1][:, 0:NH], func=AF.Sigmoid), reads=[Rps[b0 + 1]], writes=[RT[1]])
            P.op("dve", lambda e: e.tensor_tensor(out=T[0][:, hs], in0=ps[b0 + 2][:, 0:NH], in1=T[0][:, hs], op=ALU.mult),
                 reads=[Rps[b0 + 2], RT[0]], writes=[RT[0]])
            P.op("dve", lambda e: e.tensor_tensor(out=T[1][:, hs], in0=ps[b0 + 3][:, 0:NH], in1=T[1][:, hs], op=ALU.mult),
                 reads=[Rps[b0 + 3], RT[1]], writes=[RT[1]])
            P.op("dve", lambda e: e.tensor_tensor(out=merged[:, m * NT + hf * NH: m * NT + (hf + 1) * NH], in0=T[0][:, hs], in1=T[1][:, hs], op=ALU.add),
                 reads=[RT[0], RT[1]], writes=[R[f"mg{m}h{hf}"]])

        def wout_m(m, hf, bank):
            s_ = wnext()
            pairs = [(ws[s_][:, k * 128:(k + 1) * 128], merged[:, k * NT + hf * NH: k * NT + (hf + 1) * NH]) for k in range(16)]
            mm_group(bank, 128, NH, pairs, [Rws[s_]] + [R[f"mg{k}h{hf}"] for k in range(16)])
            wrel(1)
            P.op("dve", lambda e: e.tensor_tensor(out=h[:, m * NT + hf * NH: m * NT + (hf + 1) * NH], in0=ps[bank][:, 0:NH],
                                                  in1=h[:, m * NT + hf * NH: m * NT + (hf + 1) * NH], op=ALU.add),
                 reads=[Rps[bank], Rh[m]], writes=[Rh[m]])

        def mixer(it):
            p0 = it * NT
            rmsnorm(CGM)
            for i in range(4):
                P.dma("sp", T[3 + i][:, 0:NT], TAB[i][:, p0:p0 + NT], f"ld_t{i}", writes=[RT[3 + i]])
            sl = [wnext() for _ in range(3)]
            for j, (o, S) in enumerate(SUBS):
                g = it * 6 + j
                bank = j % 4
                pairs = [(u[:, k * NT + o: k * NT + o + S], ws[sl[k // 6]][:, (k % 6) * 272:(k % 6 + 1) * 272]) for k in range(16)]
                if j == 0:
                    for k in range(16):
                        P.op("pe", lambda e, k=k, bank=bank, S=S, l=pairs[k][0], r=pairs[k][1]: e.matmul(
                            ps[bank][0:S, 0:272], lhsT=l, rhs=r, start=(k == 0), stop=(k == 15)),
                            reads=[Rws[sl[k // 6]], Ru[k]], writes=[Rps[bank]])
                else:
                    mm_group(bank, S, 272, pairs, [Rws[s] for s in sl] + [R["u"]])
                P.op("act", lambda e, S=S, g=g, bank=bank: e.activation(out=vtok[0:S, g * 256:(g + 1) * 256], in_=ps[bank][0:S, 0:256], func=AF.Copy),
                     reads=[Rps[bank]], writes=[R["vtok"]])
                P.op("dve", lambda e, S=S, j=j, bank=bank: e.tensor_copy(out=wtok[0:S, j * 16:(j + 1) * 16], in_=ps[bank][0:S, 256:272]),
                     reads=[Rps[bank]], writes=[R["wtok"]])
            wrel(3)
            chunks = [(True, CKG, 0, 0, kT, R["kT"], c * TPOS + p0) for c in range(2)]
            chunks.append((False, 0, 1, 1, kiT, R["kiT"], p0))
            chunks += [(True, CQG, 0, 0, qT, [R[f"qT{c}h0"], R[f"qT{c}h1"]], c * NT) for c in range(8)]
            chunks += [(False, 0, 1, 1, qiT, [R[f"qiT{c}h0"], R[f"qiT{c}h1"]], c * NT) for c in range(8)]
            proj_chunk(wnext(), 0)
            for i, ch in enumerate(chunks):
                if i + 1 < len(chunks):
                    proj_chunk(wnext(), ((i + 1) % 2) * 4)
                rope_chunk((i % 2) * 4, *ch, i % 2)

            conv_left = list(range(8))
            def est_steps(pi):
                act_ = [2 * pi + q for q in range(2) if needs_bis(it, 2 * pi + q)]
                if not act_:
                    return 1
                return sum((it * NT + SUBS[j][0] + SUBS[j][1] + 511) // 512 for j in act_) + 1 + NBIS
            interval = max(4, (est_steps(0) + est_steps(1)) // 8)
            rnd = {"n": 0}

            def pair_steps(pi, banks):
                act = [(2 * pi + q, q) for q in range(2) if needs_bis(it, 2 * pi + q)]
                for j, q in act:
                    yield from idx_scores(it, j, q, banks)
                for j, q in act:
                    bis_setup(it, j, q)
                yield
                if act:
                    for i in range(NBIS):
                        for j, q in act:
                            bis_iter(it, j, q, i)
                        yield

            for pi in range(2):
                for _ in pair_steps(pi, (4, 5, 6, 7)):
                    if conv_left:
                        rnd["n"] += 1
                        if rnd["n"] % interval == 0:
                            conv_chunk(conv_left.pop(0))
                if pi == 1:
                    while conv_left:
                        conv_chunk(conv_left.pop(0))
                bis_final(it, 2 * pi, 0); bis_final(it, 2 * pi + 1, 1)
            for hd in range(8):
                attn_group(it, 0, hd)
            def half0_steps():
                for m in range(16):
                    merge_m(m, 0, 0)
                    yield
                for m in range(16):
                    wout_m(m, 0, m % 4)
                    yield
            ga_, gb_ = pair_steps(2, (4, 5, 6, 7)), half0_steps()
            da = db = False
            while not (da and db):
                if not da:
                    try:
                        next(ga_)
                    except StopIteration:
                        da = True
                if not db:
                    try:
                        next(gb_)
                    except StopIteration:
                        db = True
            bis_final(it, 4, 0); bis_final(it, 5, 1)
            for hd in range(8):
                attn_group(it, 1, hd)
            if dbg and it == 0:
                dump(yab[:, 0:NT], R["yab"], NT)
            for m in range(16):
                merge_m(m, 1, (m % 2) * 4)
            for m in range(16):
                wout_m(m, 1, m % 8)

        h3 = h.rearrange("p (c t) -> p c t", c=16)
        for it in range(n_tiles):
            p0 = it * NT
            for c in range(16):
                P.dma("sp", h[:, c * NT:(c + 1) * NT], xT[:, c, p0:p0 + NT], f"ld_x{c}", writes=[Rh[c]])
            ffn(0, CG1)
            if dbg and it == 0:
                dump(h[:, 0:NT], R["h"], NT)
            mixer(it)
            if dbg and it == 0:
                dump(h[:, 0:NT], R["h"], NT)
            ffn(1, CG2, store_tile=it)
        fin = [(f"st_o{c}", P.cnt[f"st_o{c}"]) for c in range(16)]
        if dbg:
            fin.append(("st_dbg", P.cnt.get("st_dbg", 0)))
        P.wait("sp", [d for d in fin if d[1] > 0])
        P.emit()
    return nc


def _fm_tiles(W):
    K, N = W.shape
    return np.ascontiguousarray(W.reshape(K // 128, 128, N // 128, 128).transpose(2, 1, 0, 3).reshape(N // 128, 128, (K // 128) * 128))


def _rope_tables():
    f32 = np.float32
    pos = np.arange(TPOS, dtype=f32)

    def tab(rot_dim, blk):
        half = rot_dim // 2
        inv = (f32(500000.0) ** (-(np.arange(0, rot_dim, 2, dtype=f32)) / f32(rot_dim))).astype(f32)
        ang = (pos[:, None] * inv[None, :]).astype(f32)
        cos = np.cos(ang).astype(f32).T
        sin = np.sin(ang).astype(f32).T
        ct = np.ones((128, TPOS), f32)
        st = np.zeros((128, TPOS), f32)
        for b in range(0, 128, blk):
            ct[b:b + half] = cos
            ct[b + half:b + 2 * half] = cos
            st[b:b + half] = -sin
            st[b + half:b + 2 * half] = sin
        return ct, st

    cA, sA = tab(32, 128)
    cI, sI = tab(16, 64)
    return np.stack([cA, sA, cI, sI]).astype(f32)


def _perms():
    f32 = np.float32
    pa = np.zeros((128, 128), f32)
    for m in range(16):
        pa[m + 16, m] = 1.0
        pa[m, m + 16] = 1.0
    pi = np.zeros((128, 128), f32)
    for b in (0, 64):
        for m in range(8):
            pi[b + m + 8, b + m] = 1.0
            pi[b + m, b + m + 8] = 1.0
    return np.stack([pa, pi, np.eye(128, dtype=f32)])


def prep_shared(inp):
    f32 = np.float32
    g = lambda k: np.asarray(inp[k], f32)[0]
    sh = {}
    for i, pre in ((1, "ffn1"), (2, "ffn2")):
        sh[f"wg{i}"] = _fm_tiles(g(f"{pre}_w_gate"))
        sh[f"wu{i}"] = _fm_tiles(g(f"{pre}_w_up"))
        sh[f"wd{i}"] = _fm_tiles(g(f"{pre}_w_down"))
    win = g("w_in")
    sp = np.cumsum([0, 1024, 256, 256, 1024, 64, 16, 1024, 1024, 1024, 2048, 2048])
    seg = lambda i: win[:, sp[i]:sp[i + 1]]
    q, k, v, qi, ki, wi, xc, gb, gc, ga, gc2 = [seg(i) for i in range(11)]
    fm = np.concatenate([k, ki, ki, q, qi, xc, gb, gc, ga, gc2], axis=1)
    assert fm.shape[1] == N_WIN * 128
    sh["win"] = _fm_tiles(fm)
    vw = np.concatenate([v, wi], axis=1)
    vw = vw.reshape(16, 128, 272).transpose(1, 0, 2)
    vwp = np.zeros((128, 18, 272), f32)
    vwp[:, :16] = vw
    sh["wvw"] = np.ascontiguousarray(vwp.reshape(128, 3, 6 * 272).transpose(1, 0, 2))
    sh["wat"] = _fm_tiles(g("w_attn_branch"))
    sh["wcv"] = _fm_tiles(g("w_conv_branch"))
    sh["wout"] = _fm_tiles(g("w_out"))
    cst = np.zeros((128, NCST), f32)
    cst[:, CG1:CG1 + 16] = g("ffn1_norm_g").reshape(16, 128).T
    cst[:, CGM:CGM + 16] = g("mix_norm_g").reshape(16, 128).T
    cst[:, CG2:CG2 + 16] = g("ffn2_norm_g").reshape(16, 128).T
    cst[:, CQG] = g("q_norm_g")
    cst[:, CKG] = g("k_norm_g")
    cw = g("conv_w")
    cst[:, CCW:CCW + 24] = cw.reshape(3, 8, 128).transpose(2, 1, 0).reshape(128, 24)
    cst[:, CCB:CCB + 8] = g("conv_b").reshape(8, 128).T
    cst[:, CPW:CPW + 32] = (2.0 ** -(np.arange(32, dtype=np.float64) + 1.0)).astype(f32)[None, :]
    sh["cst"] = cst
    sh["tab"] = _rope_tables()
    sh["prm"] = _perms()
    return sh


def prep_x(inp, b):
    f32 = np.float32
    h0 = np.concatenate([np.asarray(inp["meta_tokens"], f32), np.asarray(inp["x"][b], f32)], axis=0)
    return np.ascontiguousarray(h0.reshape(TPOS, 16, 128).transpose(2, 1, 0))


def kernel(**inputs):
    sh = prep_shared(inputs)
    nb = inputs["x"].shape[0]
    nc = build()
    in_maps = []
    for b in range(nb):
        m = dict(sh)
        m["xT"] = prep_x(inputs, b)
        in_maps.append(m)
    res = run_bass_kernel_spmd(nc, in_maps, core_ids=list(range(nb)))
    out = np.empty((nb, 2048, D), np.float32)
    for b in range(nb):
        o = res.results[b]["outT"]
        out[b] = o.transpose(2, 1, 0).reshape(2048, D)
    return out
```

```python
import math
from contextlib import ExitStack

import numpy as np
import concourse.bass as bass
import concourse.mybir as mybir
from concourse.bass_utils import run_bass_kernel_spmd

F32 = mybir.dt.float32
BF16 = mybir.dt.bfloat16
AF = mybir.ActivationFunctionType
ALU = mybir.AluOpType

D = 2048
DFF = 5632
NFF = DFF // 128
TPOS = 2064
NT = 688
NH = 344
NTILE = 3
SUBS = [(0, 128), (128, 128), (256, 88), (344, 128), (472, 128), (600, 88)]
NSUB = 18
NSLOT = 6
EPS = 1e-6
NEG = -30000.0
ENGS = ["pe", "act", "dve", "pool", "sp"]

C_K, C_KI, C_Q, C_QI, C_XC, C_GB, C_GC, C_GA, C_GC2 = 0, 2, 3, 11, 19, 27, 35, 43, 59
N_WIN = 75
CG1, CGM, CG2, CQG, CKG, CCW, CCB = 0, 16, 32, 48, 49, 50, 74
CPW = 82
NCST = 114
NBIS = 24


class Res:
    __slots__ = ("name", "writer", "readers", "ov")

    def __init__(self, name):
        self.name = name
        self.writer = None
        self.readers = {}
        self.ov = [self]


def alias(*rs):
    for a in rs:
        for b in rs:
            if b not in a.ov:
                a.ov.append(b)


class Prog:
    def __init__(self, nc, ctx):
        self.nc = nc
        self.ctx = ctx
        self.ops = {e: [] for e in ENGS}
        self.sems = {}
        self.cnt = {}
        self.waited = {e: {} for e in ENGS}
        for e in ENGS:
            self._sem("E_" + e)

    def _sem(self, key):
        if key not in self.sems:
            self.sems[key] = self.ctx.enter_context(self.nc.semaphore(key))
            self.cnt[key] = 0
        return self.sems[key]

    def _collect(self, eng, reads, writes, extra):
        deps = {}

        def add(d):
            if d is None:
                return
            k, v = d
            if deps.get(k, 0) < v:
                deps[k] = v

        for r in reads:
            for y in r.ov:
                add(y.writer)
        for w in writes:
            for y in w.ov:
                add(y.writer)
                for k, v in y.readers.items():
                    add((k, v))
        for d in extra:
            add(d)
        waits = []
        wd = self.waited[eng]
        for k, v in deps.items():
            if wd.get(k, 0) < v:
                wd[k] = v
                waits.append((self.sems[k], v))
        return waits

    def _commit(self, tick, reads, writes):
        for r in reads:
            if r not in writes:
                k, v = tick
                if r.readers.get(k, 0) < v:
                    r.readers[k] = v
        for w in writes:
            w.writer = tick
            w.readers = {}

    def op(self, eng, fn, reads=(), writes=(), extra=()):
        waits = self._collect(eng, reads, writes, extra)
        key = "E_" + eng
        self.cnt[key] += 1
        tick = (key, self.cnt[key])
        self.ops[eng].append((fn, waits, (self.sems[key], 1)))
        self._commit(tick, reads, writes)
        return tick

    def dma(self, queue, out, in_, sem, reads=(), writes=(), extra=(), **kw):
        self._sem(sem)
        waits = self._collect(queue, reads, writes, extra)
        self.cnt[sem] += 16
        tick = (sem, self.cnt[sem])

        def fn(e, out=out, in_=in_, kw=kw):
            return e.dma_start(out=out, in_=in_, **kw)

        self.ops[queue].append((fn, waits, (self.sems[sem], 16)))
        self._commit(tick, reads, writes)
        return tick

    def wait(self, eng, deps):
        waits = self._collect(eng, (), (), deps)
        self.ops[eng].append((None, waits, None))

    def emit(self):
        nc = self.nc
        with nc.Block() as block:
            def replay(name):
                def run(e):
                    for fn, waits, inc in self.ops[name]:
                        for s, v in waits:
                            e.wait_ge(s, v)
                        if fn is None:
                            continue
                        ins = fn(e)
                        if inc is not None:
                            ins.then_inc(inc[0], inc[1])
                return run
            block.tensor(replay("pe"))
            block.scalar(replay("act"))
            block.vector(replay("dve"))
            block.gpsimd(replay("pool"))
            block.sync(replay("sp"))


def build(n_tiles=NTILE, dbg=None):
    nc = bass.Bass("TRN2", target_bir_lowering=False)
    dram = lambda n, s: nc.dram_tensor(n, s, F32, kind="ExternalInput").ap()
    xT = dram("xT", [128, 16, TPOS])
    WG = [dram("wg1", [NFF, 128, 2048]), dram("wg2", [NFF, 128, 2048])]
    WU = [dram("wu1", [NFF, 128, 2048]), dram("wu2", [NFF, 128, 2048])]
    WD = [dram("wd1", [16, 128, DFF]), dram("wd2", [16, 128, DFF])]
    WIN = dram("win", [N_WIN, 128, 2048])
    WVW = dram("wvw", [3, 128, 6 * 272])
    WAT = dram("wat", [16, 128, 1024])
    WCV = dram("wcv", [16, 128, 1024])
    WOUT = dram("wout", [16, 128, 2048])
    CST = dram("cst", [128, NCST])
    TAB = dram("tab", [4, 128, TPOS])
    PRM = dram("prm", [3, 128, 128])
    outT = nc.dram_tensor("outT", [128, 16, 2048], F32, kind="ExternalOutput").ap()
    dbg_out = None
    if dbg:
        dbg_out = nc.dram_tensor("dbg", [128, dbg], F32, kind="ExternalOutput").ap()

    with ExitStack() as ctx:
        P = Prog(nc, ctx)
        layout = {}
        cur = [0]

        def carve(name, nbytes, at=None):
            nb = (nbytes + 63) // 64 * 64
            lo = cur[0] if at is None else at
            layout[name] = (lo, lo + nb)
            if at is None:
                cur[0] += nb
            return lo

        carve("h", 16 * NT * 4)
        carve("u", 16 * NT * 2)
        for i in range(NSLOT):
            carve(f"ws{i}", 4096)
        carve("kT", 2 * TPOS * 2)
        carve("vtok", NSUB * 256 * 2)
        carve("kiT", TPOS * 2)
        carve("cst", NCST * 4)
        carve("perm", 2 * 128 * 4)
        carve("ident", 128 * 2)
        carve("ones", 128 * 2)
        carve("negc", 128 * 2)
        carve("epsc", 64)
        carve("halo", 8 * 2 * 4)
        carve("wtok", 6 * 16 * 4)
        for i in range(2):
            carve(f"bsc{i}", 64 * 4)
            carve(f"pw{i}", 32 * 4)
        for i in range(7):
            carve(f"T{i}", 696 * 4)
        for i in range(2):
            carve(f"sq{i}", NT * 2)
        for i in range(2):
            carve(f"rl{i}", 512 * 4)
        carve("mb3", TPOS * 2)
        for i in range(4):
            carve(f"pt{i}", NH * 2)
        carve("ycx0", NT * 2); carve("ycx1", NT * 2)
        a_lo = carve("a", NFF * NT * 2)
        off = a_lo
        for nm, nb in (("qT", 8 * NT * 2), ("qiT", 8 * NT * 2), ("yab", 8 * NT * 2),
                       ("isc0", TPOS * 4), ("isc1", TPOS * 4), ("mb0", TPOS * 2), ("mb1", TPOS * 2)):
            carve(nm, nb, at=off)
            off = layout[nm][1]
        assert off <= layout["a"][1], (off, layout["a"])
        layout["merged"] = (layout["qT"][0], layout["qiT"][1])
        layout["mb2"] = (layout["T1"][0], layout["T1"][0] + 4160)
        assert layout["mb2"][1] <= layout["T2"][1]
        layout["junk"] = (layout["T3"][0], layout["T3"][0] + 4160)
        assert layout["junk"][1] <= layout["T4"][1]
        for i in range(3):
            layout[f"Tb{i}"] = (layout["isc0"][0] + i * NT * 4, layout["isc0"][0] + (i + 1) * NT * 4)
        assert layout["Tb2"][1] <= layout["isc0"][1]
        for pre in ("T", "Tb"):
            for i in range(3):
                for hf in range(2):
                    lo = layout[f"{pre}{i}"][0] + hf * NH * 4
                    layout[f"{pre}{i}h{hf}"] = (lo, lo + NH * 4)
        for c in range(8):
            for hf in range(2):
                for nm in ("qT", "qiT"):
                    lo = layout[nm][0] + c * NT * 2 + hf * NH * 2
                    layout[f"{nm}{c}h{hf}"] = (lo, lo + NH * 2)
        for m in range(16):
            for hf in range(2):
                lo = layout["merged"][0] + m * NT * 2 + hf * NH * 2
                layout[f"mg{m}h{hf}"] = (lo, lo + NH * 2)
        for c in range(16):
            layout[f"h{c}"] = (layout["h"][0] + c * NT * 4, layout["h"][0] + (c + 1) * NT * 4)
            layout[f"u{c}"] = (layout["u"][0] + c * NT * 2, layout["u"][0] + (c + 1) * NT * 2)
        for p_ in range(2):
            for hf in range(2):
                lo = layout[f"sq{p_}"][0] + hf * NH * 2
                layout[f"sq{p_}h{hf}"] = (lo, lo + NH * 2)
        homes = [layout["T5"][0], layout["T5"][0] + NT * 2, layout["T6"][0], layout["T6"][0] + NT * 2,
                 layout["sq0"][0], layout["sq1"][0], layout["ycx0"][0], layout["ycx1"][0]]
        for c, lo in enumerate(homes):
            layout[f"ycb{c}"] = (lo, lo + NT * 2)
        total = cur[0]
        assert total <= 212800, total
        arena = ctx.enter_context(nc.sbuf_tensor("arena", [128, total // 2], BF16))

        def vb(name):
            lo, hi = layout[name]
            return arena[:, lo // 2: hi // 2]

        def vf(name):
            lo, hi = layout[name]
            return arena[:, lo // 2: hi // 2].bitcast(F32)

        R = {n: Res(n) for n in layout}
        names = list(layout)
        for i, a_ in enumerate(names):
            for b_ in names[i + 1:]:
                la, lb = layout[a_], layout[b_]
                if la[0] < lb[1] and lb[0] < la[1]:
                    alias(R[a_], R[b_])

        h = vf("h"); u = vb("u"); a = vb("a")
        ws = [vb(f"ws{i}") for i in range(NSLOT)]
        kT = vb("kT"); vtok = vb("vtok"); kiT = vb("kiT")
        cst = vf("cst"); perm = vf("perm"); ident = vb("ident"); ones = vb("ones"); negc = vb("negc")
        epsc = vf("epsc"); halo = vf("halo"); wtok = vf("wtok")
        bsc = [vf("bsc0"), vf("bsc1")]; pw = [vf("pw0"), vf("pw1")]
        Rbsc = [R["bsc0"], R["bsc1"]]; Rpw = [R["pw0"], R["pw1"]]
        T = [vf(f"T{i}") for i in range(7)]
        sq = [vb(f"sq{i}") for i in range(2)]
        rl = [vf(f"rl{i}") for i in range(2)]
        pt = [vb(f"pt{i}") for i in range(4)]
        qT = vb("qT"); qiT = vb("qiT"); yab = vb("yab")
        iscs = [vf("isc0"), vf("isc1")]; Risc = [R["isc0"], R["isc1"]]
        junk = vb("junk")
        mb = [vb(f"mb{i}") for i in range(4)]
        MBI = [0, 1, 2, 3, 0, 1]
        merged = vb("merged")
        ycbv = [vb(f"ycb{c}") for c in range(8)]
        Rycb = [R[f"ycb{c}"] for c in range(8)]
        TT = [[T[0], T[1], T[2]], [vf("Tb0"), vf("Tb1"), vf("Tb2")]]
        TPRE = ["T", "Tb"]
        RT = [R[f"T{i}"] for i in range(7)]
        Rsq = [R[f"sq{i}"] for i in range(2)]
        Rrl = [R[f"rl{i}"] for i in range(2)]
        Rpt = [R[f"pt{i}"] for i in range(4)]
        Rws = [R[f"ws{i}"] for i in range(NSLOT)]
        Rmb = [R[f"mb{i}"] for i in range(4)]

        Rh = [R[f"h{c}"] for c in range(16)]
        Ru = [R[f"u{c}"] for c in range(16)]
        ps = [ctx.enter_context(nc.psum_tensor(f"ps{i}", [128, 512], F32)) for i in range(8)]
        Rps = [Res(f"ps{i}") for i in range(8)]

        dbg_col = [0]

        def dump(ap, res, ncols):
            if dbg_out is None:
                return
            c0 = dbg_col[0]
            np_ = ap.shape[0]
            P.dma("pool", dbg_out[0:np_, c0:c0 + ncols], ap, "st_dbg", reads=[res])
            dbg_col[0] += ncols
            return c0

        wq = []
        wstate = {"issued": 0, "used": 0, "rel": 0}

        def wplan(ap, ncols):
            wq.append((ap, ncols))

        def wissue_upto(k):
            while wstate["issued"] < min(k, len(wq)):
                i = wstate["issued"]
                ap, ncols = wq[i]
                s = i % NSLOT
                P.dma("pool", ws[s][:, 0:ncols], ap, f"ld_w{s}", writes=[Rws[s]], max_dma_last_dim=4096)
                wstate["issued"] += 1

        def wnext():
            i = wstate["used"]
            wissue_upto(wstate["rel"] + NSLOT)
            assert wstate["issued"] > i, "weight ring over-subscribed"
            wstate["used"] += 1
            return i % NSLOT

        def wrel(n):
            wstate["rel"] += n
            assert wstate["rel"] <= wstate["used"]
            wissue_upto(wstate["rel"] + NSLOT)

        P.dma("sp", cst[:, 0:NCST], CST[:, :], "ld_c0", writes=[R["cst"]])
        P.dma("sp", perm[:, 0:128], PRM[0], "ld_c1", writes=[R["perm"]])
        P.dma("sp", perm[:, 128:256], PRM[1], "ld_c2", writes=[R["perm"]])
        P.dma("pool", ident[:, :], PRM[2], "ld_id", writes=[R["ident"]])
        P.op("pool", lambda e: e.memset(ones[:, :], 1.0), writes=[R["ones"]])
        P.op("pool", lambda e: e.memset(negc[:, :], NEG), writes=[R["negc"]])
        P.op("pool", lambda e: e.memset(epsc[:, :], EPS), writes=[R["epsc"]])
        P.op("pool", lambda e: e.memset(halo[:, :], 0.0), writes=[R["halo"]])

        def mm_group(bank, M, N, pairs, reads, col0=0):
            def fn(e, bank=bank, M=M, N=N, pairs=pairs, col0=col0):
                ins = None
                n = len(pairs)
                for i, (l, r) in enumerate(pairs):
                    ins = e.matmul(ps[bank][0:M, col0:col0 + N], lhsT=l, rhs=r, start=(i == 0), stop=(i == n - 1))
                return ins
            return P.op("pe", fn, reads=reads, writes=[Rps[bank]])

        def rmsnorm(gcol):
            for c in range(16):
                P.op("act", lambda e, c=c: e.activation(out=sq[c % 2][:, 0:NT], in_=h[:, c * NT:(c + 1) * NT], func=AF.Square),
                     reads=[Rh[c]], writes=[Rsq[c % 2]])
                for hf in range(2):
                    def fn(e, c=c, hf=hf):
                        return e.matmul(ps[hf][:, 0:NH], lhsT=ones[:, :], rhs=sq[c % 2][:, hf * NH:(hf + 1) * NH],
                                        start=(c == 0), stop=(c == 15))
                    P.op("pe", fn, reads=[Rsq[c % 2], R["ones"]], writes=[Rps[hf]])
            for hf in range(2):
                P.op("act", lambda e, hf=hf: e.activation(out=T[2][:, hf * NH:(hf + 1) * NH], in_=ps[hf][:, 0:NH], func=AF.Ln,
                                                          bias=epsc[:, 0:1], scale=1.0 / D),
                     reads=[Rps[hf], R["epsc"]], writes=[RT[2]])
            P.op("act", lambda e: e.activation(out=T[2][:, 0:NT], in_=T[2][:, 0:NT], func=AF.Exp, scale=-0.5), reads=[RT[2]], writes=[RT[2]])
            for c in range(16):
                P.op("dve", lambda e, c=c: e.scalar_tensor_tensor(out=u[:, c * NT:(c + 1) * NT], in0=h[:, c * NT:(c + 1) * NT],
                                                                  scalar=cst[:, gcol + c:gcol + c + 1], in1=T[2][:, 0:NT],
                                                                  op0=ALU.mult, op1=ALU.mult),
                     reads=[Rh[c], RT[2], R["cst"]], writes=[Ru[c]])

        def ffn(which, gcol, store_tile=None):
            rmsnorm(gcol)
            for j in range(NFF):
                sg_ = wnext(); su_ = wnext()
                b0 = (j % 2) * 4
                for (s_, bb) in ((sg_, b0), (su_, b0 + 2)):
                    for hf in range(2):
                        pairs = [(ws[s_][:, k * 128:(k + 1) * 128], u[:, k * NT + hf * NH: k * NT + (hf + 1) * NH]) for k in range(16)]
                        if j == 0 and s_ == sg_ and hf == 0:
                            for k in range(16):
                                P.op("pe", lambda e, k=k, bank=bb + hf, l=pairs[k][0], r=pairs[k][1]: e.matmul(
                                    ps[bank][:, 0:NH], lhsT=l, rhs=r, start=(k == 0), stop=(k == 15)),
                                    reads=[Rws[s_], Ru[k]], writes=[Rps[bb + hf]])
                        else:
                            mm_group(bb + hf, 128, NH, pairs, [Rws[s_], R["u"]])
                wrel(2)
                q_ = j % 2
                for hf in range(2):
                    P.op("act", lambda e, q_=q_, hf=hf, b=b0 + hf: e.activation(out=T[q_][:, hf * NH:(hf + 1) * NH], in_=ps[b][:, 0:NH], func=AF.Silu),
                         reads=[Rps[b0 + hf]], writes=[RT[q_]])
                    P.op("dve", lambda e, q_=q_, hf=hf, b=b0 + 2 + hf, j=j: e.tensor_tensor(
                        out=a[:, j * NT + hf * NH: j * NT + (hf + 1) * NH], in0=ps[b][:, 0:NH], in1=T[q_][:, hf * NH:(hf + 1) * NH], op=ALU.mult),
                        reads=[Rps[b0 + 2 + hf], RT[q_]], writes=[R["a"]])
            for m in range(16):
                sl = [wnext() for _ in range(3)]
                for hf in range(2):
                    bank = (m % 4) * 2 + hf
                    pairs = [(ws[sl[hc // 16]][:, (hc % 16) * 128:(hc % 16 + 1) * 128], a[:, hc * NT + hf * NH: hc * NT + (hf + 1) * NH])
                             for hc in range(NFF)]
                    mm_group(bank, 128, NH, pairs, [Rws[s] for s in sl] + [R["a"]])
                    if hf == 1:
                        wrel(3)
                    P.op("dve", lambda e, m=m, hf=hf, bank=bank: e.scalar_tensor_tensor(
                        out=h[:, m * NT + hf * NH: m * NT + (hf + 1) * NH], in0=ps[bank][:, 0:NH], scalar=0.5,
                        in1=h[:, m * NT + hf * NH: m * NT + (hf + 1) * NH], op0=ALU.mult, op1=ALU.add),
                        reads=[Rps[bank], Rh[m]], writes=[Rh[m]])
                if store_tile is not None:
                    p0_ = store_tile * NT
                    if store_tile == 0:
                        P.dma("sp", outT[:, m, 0:NT - 16], h[:, m * NT + 16:(m + 1) * NT], f"st_o{m}", reads=[Rh[m]])
                    else:
                        P.dma("sp", outT[:, m, p0_ - 16:p0_ - 16 + NT], h[:, m * NT:(m + 1) * NT], f"st_o{m}", reads=[Rh[m]])

        def plan_ffn(which):
            for j in range(NFF):
                wplan(WG[which][j], 2048); wplan(WU[which][j], 2048)
            for m in range(16):
                wplan(WD[which][m][:, 0:2048], 2048); wplan(WD[which][m][:, 2048:4096], 2048); wplan(WD[which][m][:, 4096:DFF], 1536)

        def plan_mixer():
            for i in range(3):
                wplan(WVW[i], 6 * 272)
            for c in (C_K, C_K + 1, C_KI):
                wplan(WIN[c], 2048)
            for c in range(8):
                wplan(WIN[C_Q + c], 2048)
            for c in range(8):
                wplan(WIN[C_QI + c], 2048)
            for c in range(8):
                wplan(WIN[C_XC + c], 2048); wplan(WIN[C_GC + c], 2048); wplan(WIN[C_GB + c], 2048)
            for hf in range(2):
                for m in range(16):
                    wplan(WIN[C_GA + m], 2048); wplan(WIN[C_GC2 + m], 2048); wplan(WAT[m], 1024); wplan(WCV[m], 1024)
                for m in range(16):
                    wplan(WOUT[m], 2048)

        for it in range(n_tiles):
            plan_ffn(0); plan_mixer(); plan_ffn(1)

        def proj_chunk(slot, bank0):
            for hf in range(2):
                pairs = [(ws[slot][:, k * 128:(k + 1) * 128], u[:, k * NT + hf * NH: k * NT + (hf + 1) * NH]) for k in range(16)]
                mm_group(bank0 + hf, 128, NH, pairs, [Rws[slot], R["u"]])
            wrel(1)

        SHUF = [list(range(16, 32)) + list(range(0, 16)), list(range(8, 16)) + list(range(0, 8)) + list(range(16, 32))]

        def rope_chunk(bank0, do_norm, gcol, tabi, permi, dst, dst_res, dst_col, par):
            A, B, C = TT[par]
            for hf in range(2):
                RA, RB, RC = [R[f"{TPRE[par]}{i}h{hf}"] for i in range(3)]
                Rq = R[f"sq{par}h{hf}"]
                sqv = sq[par]
                b = bank0 + hf
                b2 = bank0 + 2 + hf
                hs = slice(hf * NH, (hf + 1) * NH)
                if do_norm:
                    P.op("act", lambda e, hs=hs, b=b: e.activation(out=sqv[:, hs], in_=ps[b][:, 0:NH], func=AF.Square),
                         reads=[Rps[b]], writes=[Rq])
                    P.op("pe", lambda e, hs=hs, b2=b2: e.matmul(ps[b2][:, 0:NH], lhsT=ones[:, :], rhs=sqv[:, hs], start=True, stop=True),
                         reads=[Rq, R["ones"]], writes=[Rps[b2]])
                    P.op("act", lambda e, hs=hs, b2=b2: e.activation(out=A[:, hs], in_=ps[b2][:, 0:NH], func=AF.Ln, bias=epsc[:, 0:1], scale=1.0 / 128),
                         reads=[Rps[b2], R["epsc"]], writes=[RA])
                    P.op("act", lambda e, hs=hs: e.activation(out=A[:, hs], in_=A[:, hs], func=AF.Exp, scale=-0.5), reads=[RA], writes=[RA])
                    P.op("dve", lambda e, hs=hs, b=b: e.scalar_tensor_tensor(out=B[:, hs], in0=ps[b][:, 0:NH], scalar=cst[:, gcol:gcol + 1],
                                                                            in1=A[:, hs], op0=ALU.mult, op1=ALU.mult),
                         reads=[Rps[b], RA, R["cst"]], writes=[RB])
                    P.op("dve", lambda e, hs=hs: e.stream_shuffle(out=C[:, hs], in_=B[:, hs], mask=SHUF[permi]), reads=[RB], writes=[RC])
                    P.op("dve", lambda e, hs=hs: e.tensor_tensor(out=B[:, hs], in0=B[:, hs], in1=T[3 + 2 * tabi][:, hs], op=ALU.mult),
                         reads=[RB, RT[3 + 2 * tabi]], writes=[RB])
                else:
                    P.op("act", lambda e, hs=hs, b=b: e.activation(out=C[:, hs], in_=ps[b][:, 0:NH], func=AF.Copy), reads=[Rps[b]], writes=[RC])
                    P.op("dve", lambda e, hs=hs: e.stream_shuffle(out=C[:, hs], in_=C[:, hs], mask=SHUF[permi]), reads=[RC], writes=[RC])
                    P.op("dve", lambda e, hs=hs, b=b: e.tensor_tensor(out=B[:, hs], in0=ps[b][:, 0:NH], in1=T[3 + 2 * tabi][:, hs], op=ALU.mult),
                         reads=[Rps[b], RT[3 + 2 * tabi]], writes=[RB])
                P.op("dve", lambda e, hs=hs: e.tensor_tensor(out=C[:, hs], in0=C[:, hs], in1=T[4 + 2 * tabi][:, hs], op=ALU.mult),
                     reads=[RC, RT[4 + 2 * tabi]], writes=[RC])
                P.op("dve", lambda e, hs=hs, hf=hf: e.tensor_tensor(out=dst[:, dst_col + hf * NH: dst_col + (hf + 1) * NH], in0=C[:, hs], in1=B[:, hs], op=ALU.add),
                     reads=[RB, RC], writes=[dst_res[hf] if isinstance(dst_res, list) else dst_res])

        rot = {"sbank": 0, "grp": 0}

        def idx_scores(it, j, q, banks=(4, 5, 6, 7)):
            p0 = it * NT
            o, S = SUBS[j]
            L = p0 + o + S
            isc = iscs[q]
            nblk = (L + 511) // 512
            for blk in range(nblk):
                c0 = blk * 512
                W = min(512, L - c0)
                for hd in range(16):
                    bank = banks[rot["sbank"] % len(banks)]
                    r_ = rot["sbank"] % 2
                    rot["sbank"] += 1
                    pr = slice((hd % 2) * 64, (hd % 2) * 64 + 64)
                    qc = (hd // 2) * NT + o
                    P.op("pe", lambda e, bank=bank, S=S, W=W, pr=pr, qc=qc, c0=c0: e.matmul(
                        ps[bank][0:S, 0:W], lhsT=qiT[pr, qc:qc + S], rhs=kiT[pr, c0:c0 + W], start=True, stop=True),
                        reads=[R[f"qiT{hd // 2}h{j // 3}"], R["kiT"]], writes=[Rps[bank]])
                    P.op("act", lambda e, bank=bank, r_=r_, S=S, W=W: e.activation(out=rl[r_][0:S, 0:W], in_=ps[bank][0:S, 0:W], func=AF.Relu),
                         reads=[Rps[bank]], writes=[Rrl[r_]])
                    wc = j * 16 + hd
                    if hd == 0:
                        P.op("dve", lambda e, r_=r_, S=S, W=W, c0=c0, wc=wc: e.tensor_scalar(
                            out=isc[0:S, c0:c0 + W], in0=rl[r_][0:S, 0:W], scalar1=wtok[0:S, wc:wc + 1], scalar2=None, op0=ALU.mult),
                            reads=[Rrl[r_], R["wtok"]], writes=[Risc[q]])
                    else:
                        P.op("dve", lambda e, r_=r_, S=S, W=W, c0=c0, wc=wc: e.scalar_tensor_tensor(
                            out=isc[0:S, c0:c0 + W], in0=rl[r_][0:S, 0:W], scalar=wtok[0:S, wc:wc + 1],
                            in1=isc[0:S, c0:c0 + W], op0=ALU.mult, op1=ALU.add),
                            reads=[Rrl[r_], R["wtok"], Risc[q]], writes=[Risc[q]])
                yield

        def bis_setup(it, j, q):
            p0 = it * NT
            o, S = SUBS[j]
            L = p0 + o + S
            isc = iscs[q]; b = bsc[q]; Rb = Rbsc[q]
            P.op("dve", lambda e: e.max(out=b[0:S, 16:24], in_=isc[0:S, 0:L]), reads=[Risc[q]], writes=[Rb])
            P.op("act", lambda e: e.activation(out=junk[0:S, 0:L], in_=isc[0:S, 0:L], func=AF.Copy, scale=-1.0), reads=[Risc[q]], writes=[R["junk"]])
            P.op("dve", lambda e: e.max(out=b[0:S, 8:16], in_=junk[0:S, 0:L]), reads=[R["junk"]], writes=[Rb])
            P.op("pool", lambda e: e.affine_select(out=isc[0:S, L - S:L], in_=isc[0:S, L - S:L], pattern=[[-1, S]],
                                                   compare_op=ALU.is_ge, fill=-1.0e30, base=0, channel_multiplier=1),
                 reads=[Risc[q]], writes=[Risc[q]])
            P.op("dve", lambda e: e.tensor_scalar(out=b[0:S, 5:6], in0=b[0:S, 8:9], scalar1=1.0 + 2.0 ** -7, scalar2=None, op0=ALU.mult), reads=[Rb], writes=[Rb])
            P.op("dve", lambda e: e.scalar_tensor_tensor(out=b[0:S, 4:5], in0=b[0:S, 8:9], scalar=1.0 - 2.0 ** -7, in1=b[0:S, 5:6], op0=ALU.mult, op1=ALU.max),
                 reads=[Rb], writes=[Rb])
            P.op("dve", lambda e: e.tensor_tensor(out=b[0:S, 6:7], in0=b[0:S, 16:17], in1=b[0:S, 4:5], op=ALU.add), reads=[Rb], writes=[Rb])
            P.op("dve", lambda e: e.tensor_tensor(out=b[0:S, 7:8], in0=b[0:S, 16:17], in1=b[0:S, 4:5], op=ALU.subtract), reads=[Rb], writes=[Rb])
            P.op("dve", lambda e: e.tensor_scalar(out=b[0:S, 1:2], in0=b[0:S, 7:8], scalar1=0.5, scalar2=None, op0=ALU.mult), reads=[Rb], writes=[Rb])
            P.op("dve", lambda e: e.tensor_scalar(out=pw[q][0:S, 0:32], in0=cst[0:S, CPW:CPW + 32], scalar1=b[0:S, 6:7], scalar2=None, op0=ALU.mult),
                 reads=[Rb, R["cst"]], writes=[Rpw[q]])

        Rjunkd = Res("junkd")

        def bis_iter(it, j, q, i):
            p0 = it * NT
            o, S = SUBS[j]
            L = p0 + o + S
            isc = iscs[q]; b = bsc[q]; Rb = Rbsc[q]
            if q == 1:
                P.op("act", lambda e: e.activation(out=junk[0:S, 0:L], in_=isc[0:S, 0:L], func=AF.Sign, bias=b[0:S, 1:2], scale=-1.0, accum_out=b[0:S, 0:1]),
                     reads=[Risc[q], Rb], writes=[R["junk"], Rb])
                P.op("dve", lambda e: e.tensor_scalar(out=b[0:S, 2:3], in0=b[0:S, 0:1], scalar1=float(L - 511), scalar2=0.5, op0=ALU.is_le, op1=ALU.subtract),
                     reads=[Rb], writes=[Rb])
            else:
                P.op("dve", lambda e: e.tensor_scalar(out=junk[0:S, 0:L], in0=isc[0:S, 0:L], scalar1=b[0:S, 1:2], scalar2=0.0, op0=ALU.is_ge, op1=ALU.add,
                                                      accum_out=b[0:S, 0:1]),
                     reads=[Risc[q], Rb], writes=[Rjunkd, Rb])
                P.op("dve", lambda e: e.tensor_scalar(out=b[0:S, 2:3], in0=b[0:S, 0:1], scalar1=256.0, scalar2=0.5, op0=ALU.is_ge, op1=ALU.subtract),
                     reads=[Rb], writes=[Rb])
            P.op("dve", lambda e: e.scalar_tensor_tensor(out=b[0:S, 1:2], in0=b[0:S, 2:3], scalar=pw[q][0:S, i:i + 1], in1=b[0:S, 1:2], op0=ALU.mult, op1=ALU.add),
                 reads=[Rb, Rpw[q]], writes=[Rb])

        def bis_final(it, j, q):
            p0 = it * NT
            o, S = SUBS[j]
            L = p0 + o + S
            jj = MBI[j]
            if L > 256:
                isc = iscs[q]; b = bsc[q]; Rb = Rbsc[q]
                P.op("dve", lambda e: e.scalar_tensor_tensor(out=b[0:S, 3:4], in0=pw[q][0:S, NBIS:NBIS + 1], scalar=-1.0, in1=b[0:S, 1:2], op0=ALU.mult, op1=ALU.add),
                     reads=[Rb, Rpw[q]], writes=[Rb])
                P.op("dve", lambda e: e.tensor_scalar(out=mb[jj][0:S, 0:L], in0=isc[0:S, 0:L], scalar1=b[0:S, 3:4], scalar2=NEG, op0=ALU.is_lt, op1=ALU.mult),
                     reads=[Risc[q], Rb], writes=[Rmb[jj]])
            else:
                P.op("dve", lambda e: e.memset(mb[jj][0:S, 0:L], 0.0), writes=[Rmb[jj]])
            P.op("pool", lambda e: e.affine_select(out=mb[jj][0:S, L - S:L], in_=mb[jj][0:S, L - S:L], pattern=[[-1, S]],
                                                   compare_op=ALU.is_ge, fill=NEG, base=0, channel_multiplier=1),
                 reads=[Rmb[jj]], writes=[Rmb[jj]])

        def needs_bis(it, j):
            o, S = SUBS[j]
            return it * NT + o + S > 256

        def attn_group(it, hf, hd, sbanks=(4, 5, 6, 7), LA=2):
            nch = it * 6 + hf * 3 + 3
            gq = hd // 4
            ob = rot["grp"] % 2
            sb_ = 2 + rot["grp"] % 2
            rot["grp"] += 1
            q0 = hd * NT + hf * NH
            units = []
            for c in range(nch):
                ti, jc = divmod(c, 6)
                oc, Sc = SUBS[jc]
                units.append((c, ti * NT + oc, Sc))
            nu = len(units)

            def emit_qk(c, pc, Sc, k):
                bank = sbanks[k % len(sbanks)]

                def fn(e):
                    e.matmul(ps[bank][0:Sc, 0:NH], lhsT=kT[:, gq * TPOS + pc: gq * TPOS + pc + Sc], rhs=qT[:, q0:q0 + NH], start=True, stop=False)
                    ins = None
                    for jj in range(3):
                        o_, S_ = SUBS[hf * 3 + jj]
                        cs = o_ - hf * NH
                        gt = it * 6 + hf * 3 + jj
                        if c <= gt:
                            l = mb[MBI[hf * 3 + jj]][0:S_, pc:pc + Sc]
                        else:
                            l = negc[0:S_, 0:Sc]
                        ins = e.matmul(ps[bank][0:Sc, cs:cs + S_], lhsT=l, rhs=ident[0:S_, 0:S_], start=False, stop=(jj == 2))
                    return ins
                P.op("pe", fn, reads=[R["kT"], R[f"qT{hd}h{hf}"], R["ident"], R["negc"]] + Rmb, writes=[Rps[bank]])
                P.op("act", lambda e: e.activation(out=pt[k % 4][0:Sc, 0:NH], in_=ps[bank][0:Sc, 0:NH], func=AF.Exp, scale=128.0 ** -0.5),
                     reads=[Rps[bank]], writes=[Rpt[k % 4]])

            def emit_pv(c, pc, Sc, k):
                first = (k == 0)
                last = (k == nu - 1)

                def fn(e):
                    e.matmul(ps[ob][:, 0:NH], lhsT=vtok[0:Sc, c * 256 + gq * 128: c * 256 + gq * 128 + 128], rhs=pt[k % 4][0:Sc, 0:NH], start=first, stop=last)
                    return e.matmul(ps[sb_][:, 0:NH], lhsT=ones[0:Sc, :], rhs=pt[k % 4][0:Sc, 0:NH], start=first, stop=last)
                P.op("pe", fn, reads=[R["vtok"], Rpt[k % 4], R["ones"]], writes=[Rps[ob], Rps[sb_]])

            for k in range(nu + LA):
                if k < nu:
                    emit_qk(*units[k], k)
                if k >= LA:
                    emit_pv(*units[k - LA], k - LA)
            P.op("act", lambda e: e.activation(out=T[0][:, 0:NH], in_=ps[sb_][:, 0:NH], func=AF.Ln), reads=[Rps[sb_]], writes=[RT[0]])
            P.op("act", lambda e: e.activation(out=T[0][:, 0:NH], in_=T[0][:, 0:NH], func=AF.Exp, scale=-1.0), reads=[RT[0]], writes=[RT[0]])
            P.op("dve", lambda e: e.tensor_tensor(out=yab[:, q0:q0 + NH], in0=ps[ob][:, 0:NH], in1=T[0][:, 0:NH], op=ALU.mult),
                 reads=[Rps[ob], RT[0]], writes=[R["yab"]])

        def conv_chunk(c):
            sx = wnext(); sc_ = wnext(); sg_ = wnext()
            proj_chunk(sx, 0); proj_chunk(sc_, 2)
            for hf in range(2):
                P.op("act", lambda e, hf=hf: e.activation(out=T[0][:, hf * NH:(hf + 1) * NH], in_=ps[hf][:, 0:NH], func=AF.Copy),
                     reads=[Rps[hf]], writes=[RT[0]])
            proj_chunk(sg_, 0)
            P.op("dve", lambda e: e.tensor_copy(out=T[1][:, 0:2], in_=halo[:, c * 2:c * 2 + 2]), reads=[R["halo"]], writes=[RT[1]])
            for hf in range(2):
                P.op("dve", lambda e, hf=hf: e.tensor_tensor(out=T[1][:, 2 + hf * NH: 2 + (hf + 1) * NH], in0=ps[2 + hf][:, 0:NH],
                                                             in1=T[0][:, hf * NH:(hf + 1) * NH], op=ALU.mult),
                     reads=[Rps[2 + hf], RT[0]], writes=[RT[1]])
            P.op("act", lambda e: e.activation(out=halo[:, c * 2:c * 2 + 2], in_=T[1][:, NT:NT + 2], func=AF.Copy),
                 reads=[RT[1]], writes=[R["halo"]])
            P.op("act", lambda e: e.activation(out=T[2][:, 0:NT], in_=T[1][:, 2:2 + NT], func=AF.Identity,
                                               bias=cst[:, CCB + c:CCB + c + 1], scale=cst[:, CCW + c * 3 + 2:CCW + c * 3 + 3]),
                 reads=[RT[1], R["cst"]], writes=[RT[2]])
            for tap in (1, 0):
                P.op("dve", lambda e, tap=tap: e.scalar_tensor_tensor(out=T[2][:, 0:NT], in0=T[1][:, tap:tap + NT],
                                                                      scalar=cst[:, CCW + c * 3 + tap:CCW + c * 3 + tap + 1],
                                                                      in1=T[2][:, 0:NT], op0=ALU.mult, op1=ALU.add),
                     reads=[RT[1], RT[2], R["cst"]], writes=[RT[2]])
            for hf in range(2):
                P.op("dve", lambda e, hf=hf: e.tensor_tensor(out=ycbv[c][:, hf * NH:(hf + 1) * NH], in0=ps[hf][:, 0:NH],
                                                             in1=T[2][:, hf * NH:(hf + 1) * NH], op=ALU.mult),
                     reads=[Rps[hf], RT[2]], writes=[Rycb[c]])

        def merge_m(m, hf, b0):
            hs = slice(hf * NH, (hf + 1) * NH)
            for bb in (b0, b0 + 1):
                s_ = wnext()
                pairs = [(ws[s_][:, k * 128:(k + 1) * 128], u[:, k * NT + hf * NH: k * NT + (hf + 1) * NH]) for k in range(16)]
                mm_group(bb, 128, NH, pairs, [Rws[s_], R["u"]])
                wrel(1)
            sat = wnext()
            pairs = [(ws[sat][:, k * 128:(k + 1) * 128], yab[:, k * NT + hf * NH: k * NT + (hf + 1) * NH]) for k in range(8)]
            mm_group(b0 + 2, 128, NH, pairs, [Rws[sat], R["yab"]])
            wrel(1)
            scv = wnext()
            pairs = [(ws[scv][:, k * 128:(k + 1) * 128], ycbv[k][:, hf * NH:(hf + 1) * NH]) for k in range(8)]
            mm_group(b0 + 3, 128, NH, pairs, [Rws[scv]] + Rycb)
            wrel(1)
            P.op("act", lambda e: e.activation(out=T[0][:, hs], in_=ps[b0][:, 0:NH], func=AF.Sigmoid), reads=[Rps[b0]], writes=[RT[0]])
            P.op("act", lambda e: e.activation(out=T[1][:, hs], in_=ps[b0 + 1][:, 0:NH], func=AF.Sigmoid), reads=[Rps[b0 + 1]], writes=[RT[1]])
            P.op("dve", lambda e: e.tensor_tensor(out=T[0][:, hs], in0=ps[b0 + 2][:, 0:NH], in1=T[0][:, hs], op=ALU.mult),
                 reads=[Rps[b0 + 2], RT[0]], writes=[RT[0]])
            P.op("dve", lambda e: e.tensor_tensor(out=T[1][:, hs], in0=ps[b0 + 3][:, 0:NH], in1=T[1][:, hs], op=ALU.mult),
                 reads=[Rps[b0 + 3], RT[1]], writes=[RT[1]])
            P.op("dve", lambda e: e.tensor_tensor(out=merged[:, m * NT + hf * NH: m * NT + (hf + 1) * NH], in0=T[0][:, hs], in1=T[1][:, hs], op=ALU.add),
                 reads=[RT[0], RT[1]], writes=[R[f"mg{m}h{hf}"]])

        def wout_m(m, hf, bank):
            s_ = wnext()
            pairs = [(ws[s_][:, k * 128:(k + 1) * 128], merged[:, k * NT + hf * NH: k * NT + (hf + 1) * NH]) for k in range(16)]
            mm_group(bank, 128, NH, pairs, [Rws[s_]] + [R[f"mg{k}h{hf}"] for k in range(16)])
            wrel(1)
            P.op("dve", lambda e: e.tensor_tensor(out=h[:, m * NT + hf * NH: m * NT + (hf + 1) * NH], in0=ps[bank][:, 0:NH],
                                                  in1=h[:, m * NT + hf * NH: m * NT + (hf + 1) * NH], op=ALU.add),
                 reads=[Rps[bank], Rh[m]], writes=[Rh[m]])

        def mixer(it):
            p0 = it * NT
            rmsnorm(CGM)
            for i in range(4):
                P.dma("sp", T[3 + i][:, 0:NT], TAB[i][:, p0:p0 + NT], f"ld_t{i}", writes=[RT[3 + i]])
            sl = [wnext() for _ in range(3)]
            for j, (o, S) in enumerate(SUBS):
                g = it * 6 + j
                bank = j % 4
                pairs = [(u[:, k * NT + o: k * NT + o + S], ws[sl[k // 6]][:, (k % 6) * 272:(k % 6 + 1) * 272]) for k in range(16)]
                if j == 0:
                    for k in range(16):
                        P.op("pe", lambda e, k=k, bank=bank, S=S, l=pairs[k][0], r=pairs[k][1]: e.matmul(
                            ps[bank][0:S, 0:272], lhsT=l, rhs=r, start=(k == 0), stop=(k == 15)),
                            reads=[Rws[sl[k // 6]], Ru[k]], writes=[Rps[bank]])
                else:
                    mm_group(bank, S, 272, pairs, [Rws[s] for s in sl] + [R["u"]])
                P.op("act", lambda e, S=S, g=g, bank=bank: e.activation(out=vtok[0:S, g * 256:(g + 1) * 256], in_=ps[bank][0:S, 0:256], func=AF.Copy),
                     reads=[Rps[bank]], writes=[R["vtok"]])
                P.op("dve", lambda e, S=S, j=j, bank=bank: e.tensor_copy(out=wtok[0:S, j * 16:(j + 1) * 16], in_=ps[bank][0:S, 256:272]),
                     reads=[Rps[bank]], writes=[R["wtok"]])
            wrel(3)
            chunks = [(True, CKG, 0, 0, kT, R["kT"], c * TPOS + p0) for c in range(2)]
            chunks.append((False, 0, 1, 1, kiT, R["kiT"], p0))
            chunks += [(True, CQG, 0, 0, qT, [R[f"qT{c}h0"], R[f"qT{c}h1"]], c * NT) for c in range(8)]
            chunks += [(False, 0, 1, 1, qiT, [R[f"qiT{c}h0"], R[f"qiT{c}h1"]], c * NT) for c in range(8)]
            proj_chunk(wnext(), 0)
            for i, ch in enumerate(chunks):
                if i + 1 < len(chunks):
                    proj_chunk(wnext(), ((i + 1) % 2) * 4)
                rope_chunk((i % 2) * 4, *ch, i % 2)

            conv_left = list(range(8))
            def est_steps(pi):
                act_ = [2 * pi + q for q in range(2) if needs_bis(it, 2 * pi + q)]
                if not act_:
                    return 1
                return sum((it * NT + SUBS[j][0] + SUBS[j][1] + 511) // 512 for j in act_) + 1 + NBIS
            interval = max(4, (est_steps(0) + est_steps(1)) // 8)
            rnd = {"n": 0}

            def pair_steps(pi, banks):
                act = [(2 * pi + q, q) for q in range(2) if needs_bis(it, 2 * pi + q)]
                for j, q in act:
                    yield from idx_scores(it, j, q, banks)
                for j, q in act:
                    bis_setup(it, j, q)
                yield
                if act:
                    for i in range(NBIS):
                        for j, q in act:
                            bis_iter(it, j, q, i)
                        yield

            for pi in range(2):
                for _ in pair_steps(pi, (4, 5, 6, 7)):
                    if conv_left:
                        rnd["n"] += 1
                        if rnd["n"] % interval == 0:
                            conv_chunk(conv_left.pop(0))
                if pi == 1:
                    while conv_left:
                        conv_chunk(conv_left.pop(0))
                bis_final(it, 2 * pi, 0); bis_final(it, 2 * pi + 1, 1)
            for hd in range(8):
                attn_group(it, 0, hd)
            def half0_steps():
                for m in range(16):
                    merge_m(m, 0, 0)
                    yield
                for m in range(16):
                    wout_m(m, 0, m % 4)
                    yield
            ga_, gb_ = pair_steps(2, (4, 5, 6, 7)), half0_steps()
            da = db = False
            while not (da and db):
                if not da:
                    try:
                        next(ga_)
                    except StopIteration:
                        da = True
                if not db:
                    try:
                        next(gb_)
                    except StopIteration:
                        db = True
            bis_final(it, 4, 0); bis_final(it, 5, 1)
            for hd in range(8):
                attn_group(it, 1, hd)
            if dbg and it == 0:
                dump(yab[:, 0:NT], R["yab"], NT)
            for m in range(16):
                merge_m(m, 1, (m % 2) * 4)
            for m in range(16):
                wout_m(m, 1, m % 8)

        h3 = h.rearrange("p (c t) -> p c t", c=16)
        for it in range(n_tiles):
            p0 = it * NT
            for c in range(16):
                P.dma("sp", h[:, c * NT:(c + 1) * NT], xT[:, c, p0:p0 + NT], f"ld_x{c}", writes=[Rh[c]])
            ffn(0, CG1)
            if dbg and it == 0:
                dump(h[:, 0:NT], R["h"], NT)
            mixer(it)
            if dbg and it == 0:
                dump(h[:, 0:NT], R["h"], NT)
            ffn(1, CG2, store_tile=it)
        fin = [(f"st_o{c}", P.cnt[f"st_o{c}"]) for c in range(16)]
        if dbg:
            fin.append(("st_dbg", P.cnt.get("st_dbg", 0)))
        P.wait("sp", [d for d in fin if d[1] > 0])
        P.emit()
    return nc


def _fm_tiles(W):
    K, N = W.shape
    return np.ascontiguousarray(W.reshape(K // 128, 128, N // 128, 128).transpose(2, 1, 0, 3).reshape(N // 128, 128, (K // 128) * 128))


def _rope_tables():
    f32 = np.float32
    pos = np.arange(TPOS, dtype=f32)

    def tab(rot_dim, blk):
        half = rot_dim // 2
        inv = (f32(500000.0) ** (-(np.arange(0, rot_dim, 2, dtype=f32)) / f32(rot_dim))).astype(f32)
        ang = (pos[:, None] * inv[None, :]).astype(f32)
        cos = np.cos(ang).astype(f32).T
        sin = np.sin(ang).astype(f32).T
        ct = np.ones((128, TPOS), f32)
        st = np.zeros((128, TPOS), f32)
        for b in range(0, 128, blk):
            ct[b:b + half] = cos
            ct[b + half:b + 2 * half] = cos
            st[b:b + half] = -sin
            st[b + half:b + 2 * half] = sin
        return ct, st

    cA, sA = tab(32, 128)
    cI, sI = tab(16, 64)
    return np.stack([cA, sA, cI, sI]).astype(f32)


def _perms():
    f32 = np.float32
    pa = np.zeros((128, 128), f32)
    for m in range(16):
        pa[m + 16, m] = 1.0
        pa[m, m + 16] = 1.0
    pi = np.zeros((128, 128), f32)
    for b in (0, 64):
        for m in range(8):
            pi[b + m + 8, b + m] = 1.0
            pi[b + m, b + m + 8] = 1.0
    return np.stack([pa, pi, np.eye(128, dtype=f32)])


def prep_shared(inp):
    f32 = np.float32
    g = lambda k: np.asarray(inp[k], f32)[0]
    sh = {}
    for i, pre in ((1, "ffn1"), (2, "ffn2")):
        sh[f"wg{i}"] = _fm_tiles(g(f"{pre}_w_gate"))
        sh[f"wu{i}"] = _fm_tiles(g(f"{pre}_w_up"))
        sh[f"wd{i}"] = _fm_tiles(g(f"{pre}_w_down"))
    win = g("w_in")
    sp = np.cumsum([0, 1024, 256, 256, 1024, 64, 16, 1024, 1024, 1024, 2048, 2048])
    seg = lambda i: win[:, sp[i]:sp[i + 1]]
    q, k, v, qi, ki, wi, xc, gb, gc, ga, gc2 = [seg(i) for i in range(11)]
    fm = np.concatenate([k, ki, ki, q, qi, xc, gb, gc, ga, gc2], axis=1)
    assert fm.shape[1] == N_WIN * 128
    sh["win"] = _fm_tiles(fm)
    vw = np.concatenate([v, wi], axis=1)
    vw = vw.reshape(16, 128, 272).transpose(1, 0, 2)
    vwp = np.zeros((128, 18, 272), f32)
    vwp[:, :16] = vw
    sh["wvw"] = np.ascontiguousarray(vwp.reshape(128, 3, 6 * 272).transpose(1, 0, 2))
    sh["wat"] = _fm_tiles(g("w_attn_branch"))
    sh["wcv"] = _fm_tiles(g("w_conv_branch"))
    sh["wout"] = _fm_tiles(g("w_out"))
    cst = np.zeros((128, NCST), f32)
    cst[:, CG1:CG1 + 16] = g("ffn1_norm_g").reshape(16, 128).T
    cst[:, CGM:CGM + 16] = g("mix_norm_g").reshape(16, 128).T
    cst[:, CG2:CG2 + 16] = g("ffn2_norm_g").reshape(16, 128).T
    cst[:, CQG] = g("q_norm_g")
    cst[:, CKG] = g("k_norm_g")
    cw = g("conv_w")
    cst[:, CCW:CCW + 24] = cw.reshape(3, 8, 128).transpose(2, 1, 0).reshape(128, 24)
    cst[:, CCB:CCB + 8] = g("conv_b").reshape(8, 128).T
    cst[:, CPW:CPW + 32] = (2.0 ** -(np.arange(32, dtype=np.float64) + 1.0)).astype(f32)[None, :]
    sh["cst"] = cst
    sh["tab"] = _rope_tables()
    sh["prm"] = _perms()
    return sh


def prep_x(inp, b):
    f32 = np.float32
    h0 = np.concatenate([np.asarray(inp["meta_tokens"], f32), np.asarray(inp["x"][b], f32)], axis=0)
    return np.ascontiguousarray(h0.reshape(TPOS, 16, 128).transpose(2, 1, 0))


def kernel(**inputs):
    sh = prep_shared(inputs)
    nb = inputs["x"].shape[0]
    nc = build()
    in_maps = []
    for b in range(nb):
        m = dict(sh)
        m["xT"] = prep_x(inputs, b)
        in_maps.append(m)
    res = run_bass_kernel_spmd(nc, in_maps, core_ids=list(range(nb)))
    out = np.empty((nb, 2048, D), np.float32)
    for b in range(nb):
        o = res.results[b]["outT"]
        out[b] = o.transpose(2, 1, 0).reshape(2048, D)
    return out
```
